# Optimizing a Trainium2 kernel written in Bass

```python
import jax, jax.numpy as jnp
from jax import lax
import numpy as np

D_MODEL = 1024
BATCH = 16
SEQ = 4096
DEPTH = 4

GRID_W = 64
NA_HEADS = 8
NA_HEAD_DIM = 64
NA_WIDTH = NA_HEADS * NA_HEAD_DIM
NA_WIN_ROWS = 8
NA_WIN_COLS = 16
MLA_HEADS = 8
MLA_NOPE = 64
MLA_ROPE = 32
MLA_QK_DIM = MLA_NOPE + MLA_ROPE
MLA_V = 64
MLA_Q_LORA = 256
MLA_KV_LORA = 128
MLA_WIDTH = MLA_HEADS * MLA_V
ROPE_BASE = 10000.0
Q_BLOCK = 128
EPS = 1e-6
IN_SPLIT_SIZES = (NA_WIDTH, NA_WIDTH, NA_WIDTH, NA_WIDTH,
                  MLA_Q_LORA, MLA_KV_LORA, MLA_ROPE, MLA_WIDTH,
                  D_MODEL, D_MODEL)
D_IN = sum(IN_SPLIT_SIZES)

kernel_name = "hybrid_natten_mla_gated_encoder"


def rmsnorm(x, g):
    xf = x.astype(jnp.float32)
    y = xf * lax.rsqrt(jnp.mean(xf * xf, axis=-1, keepdims=True) + EPS)
    return (y * g.astype(jnp.float32)).astype(x.dtype)


def split_columns(p):
    parts, off = [], 0
    for size in IN_SPLIT_SIZES:
        parts.append(p[..., off:off + size])
        off += size
    return parts


def axial_rope_tables(seq_len):
    t = jnp.arange(seq_len)
    row = (t // GRID_W).astype(jnp.float32)
    col = (t % GRID_W).astype(jnp.float32)
    half = MLA_ROPE // 2
    n_freq = half // 2
    inv = jnp.power(jnp.float32(ROPE_BASE), -jnp.arange(n_freq, dtype=jnp.float32) / n_freq)
    ang = jnp.concatenate([row[:, None] * inv, col[:, None] * inv], axis=-1)
    return jnp.cos(ang), jnp.sin(ang)


def apply_rope(x, cos, sin):
    half = MLA_ROPE // 2
    x1, x2 = x[..., :half], x[..., half:]
    c, s = cos[None, :, None, :], sin[None, :, None, :]
    out = jnp.concatenate([x1 * c - x2 * s, x1 * s + x2 * c], axis=-1)
    return out.astype(x.dtype)


def neighborhood_attention(q, k, v, rel_bias):
    B, S, H, d = q.shape
    rows = S // GRID_W
    kr = min(NA_WIN_ROWS, rows)
    kw = NA_WIN_COLS
    kg = k.reshape(B, rows, GRID_W, H, d)
    vg = v.reshape(B, rows, GRID_W, H, d)
    qg = q.reshape(B, rows, GRID_W, H, d).transpose(1, 0, 2, 3, 4)
    cols = jnp.arange(GRID_W)
    col_start = jnp.clip(cols - kw // 2, 0, GRID_W - kw)
    col_idx = col_start[:, None] + jnp.arange(kw)[None, :]
    dcol = col_idx - cols[:, None] + (kw - 1)
    row_ids = jnp.arange(rows)
    row_start = jnp.clip(row_ids - kr // 2, 0, rows - kr)
    scale = d ** -0.5

    def one_row(args):
        q_row, r, rs = args
        k_band = lax.dynamic_slice_in_dim(kg, rs, kr, axis=1)
        v_band = lax.dynamic_slice_in_dim(vg, rs, kr, axis=1)
        k_win = k_band[:, :, col_idx]
        v_win = v_band[:, :, col_idx]
        s = jnp.einsum('bqhd,bnqjhd->bhqnj', q_row, k_win).astype(jnp.float32) * scale
        drow = rs + jnp.arange(kr) - r + (NA_WIN_ROWS - 1)
        bias = rel_bias[:, drow[None, :, None], dcol[:, None, :]]
        s = s + bias[None].astype(jnp.float32)
        p = jax.nn.softmax(s.reshape(B, H, GRID_W, kr * kw), axis=-1)
        p = p.reshape(B, H, GRID_W, kr, kw).astype(v.dtype)
        return jnp.einsum('bhqnj,bnqjhd->bqhd', p, v_win)

    o = lax.map(one_row, (qg, row_ids, row_start))
    return o.transpose(1, 0, 2, 3, 4).reshape(B, S, H * d)


def dense_block_attention(q, k, v):
    B, S, H, dk = q.shape
    dv = v.shape[-1]
    nblk = S // Q_BLOCK
    scale = dk ** -0.5
    qb = q.reshape(B, nblk, Q_BLOCK, H, dk).transpose(1, 0, 2, 3, 4)

    def one_block(q_blk):
        s = jnp.einsum('bqhd,bkhd->bhqk', q_blk, k).astype(jnp.float32) * scale
        p = jax.nn.softmax(s, axis=-1).astype(v.dtype)
        return jnp.einsum('bhqk,bkhd->bqhd', p, v)

    o = lax.map(one_block, qb)
    return o.transpose(1, 0, 2, 3, 4).reshape(B, S, H * dv)


def hybrid_layer(x, ln_g, w_in, na_q_norm, na_k_norm, na_rel_bias,
                 mla_cq_norm, mla_ckv_norm, w_uq, w_ukv, mla_q_norm, mla_k_norm,
                 w_o_na, w_o_mla, w_out, rope_cos, rope_sin):
    B, S, _ = x.shape
    h = rmsnorm(x, ln_g)
    proj = h @ w_in
    (na_q, na_k, na_v, na_gate, c_q, c_kv, k_pe, mla_gate,
     g_na, g_mla) = split_columns(proj)

    qa = rmsnorm(na_q.reshape(B, S, NA_HEADS, NA_HEAD_DIM), na_q_norm)
    ka = rmsnorm(na_k.reshape(B, S, NA_HEADS, NA_HEAD_DIM), na_k_norm)
    va = na_v.reshape(B, S, NA_HEADS, NA_HEAD_DIM)
    o_na = neighborhood_attention(qa, ka, va, na_rel_bias) * jax.nn.silu(na_gate)
    u_na = o_na @ w_o_na

    qb = (rmsnorm(c_q, mla_cq_norm) @ w_uq).reshape(B, S, MLA_HEADS, MLA_QK_DIM)
    kv = (rmsnorm(c_kv, mla_ckv_norm) @ w_ukv).reshape(B, S, MLA_HEADS, MLA_NOPE + MLA_V)
    k_nope, vb = kv[..., :MLA_NOPE], kv[..., MLA_NOPE:]
    k_rot = jnp.broadcast_to(k_pe[:, :, None, :], (B, S, MLA_HEADS, MLA_ROPE))
    kb = jnp.concatenate([k_nope, k_rot], axis=-1)
    qb = rmsnorm(qb, mla_q_norm)
    kb = rmsnorm(kb, mla_k_norm)
    qb = jnp.concatenate([qb[..., :MLA_NOPE], apply_rope(qb[..., MLA_NOPE:], rope_cos, rope_sin)], axis=-1)
    kb = jnp.concatenate([kb[..., :MLA_NOPE], apply_rope(kb[..., MLA_NOPE:], rope_cos, rope_sin)], axis=-1)
    o_mla = dense_block_attention(qb, kb, vb) * jax.nn.silu(mla_gate)
    u_mla = o_mla @ w_o_mla

    y = jax.nn.sigmoid(g_na) * u_na + jax.nn.sigmoid(g_mla) * u_mla
    return x + y @ w_out


def setup_inputs(seed: int = 0) -> dict:
    key = jax.random.key(seed)
    ks = jax.random.split(key, 16)
    f32 = jnp.float32

    def nrm(k, shape, scale):
        return jax.random.normal(k, shape, f32) * scale

    def gain(k, shape):
        return 1.0 + 0.02 * jax.random.normal(k, shape, f32)

    L = DEPTH
    return {
        "x": jax.random.normal(ks[0], (BATCH, SEQ, D_MODEL), f32),
        "ln_g": gain(ks[1], (L, D_MODEL)),
        "w_in": nrm(ks[2], (L, D_MODEL, D_IN), D_MODEL ** -0.5),
        "na_q_norm": gain(ks[3], (L, NA_HEAD_DIM)),
        "na_k_norm": gain(ks[4], (L, NA_HEAD_DIM)),
        "na_rel_bias": nrm(ks[5], (L, NA_HEADS, 2 * NA_WIN_ROWS - 1, 2 * NA_WIN_COLS - 1), 0.5),
        "mla_cq_norm": gain(ks[6], (L, MLA_Q_LORA)),
        "mla_ckv_norm": gain(ks[7], (L, MLA_KV_LORA)),
        "w_uq": nrm(ks[8], (L, MLA_Q_LORA, MLA_HEADS * MLA_QK_DIM), MLA_Q_LORA ** -0.5),
        "w_ukv": nrm(ks[9], (L, MLA_KV_LORA, MLA_HEADS * (MLA_NOPE + MLA_V)), MLA_KV_LORA ** -0.5),
        "mla_q_norm": gain(ks[10], (L, MLA_QK_DIM)),
        "mla_k_norm": gain(ks[11], (L, MLA_QK_DIM)),
        "w_o_na": nrm(ks[12], (L, NA_WIDTH, D_MODEL), NA_WIDTH ** -0.5),
        "w_o_mla": nrm(ks[13], (L, MLA_WIDTH, D_MODEL), MLA_WIDTH ** -0.5),
        "w_out": nrm(ks[14], (L, D_MODEL, D_MODEL), D_MODEL ** -0.5),
    }


def reference(x, ln_g, w_in, na_q_norm, na_k_norm, na_rel_bias, mla_cq_norm, mla_ckv_norm,
              w_uq, w_ukv, mla_q_norm, mla_k_norm, w_o_na, w_o_mla, w_out):
    rope_cos, rope_sin = axial_rope_tables(x.shape[1])
    for l in range(DEPTH):
        x = hybrid_layer(x, ln_g[l], w_in[l], na_q_norm[l], na_k_norm[l], na_rel_bias[l],
                         mla_cq_norm[l], mla_ckv_norm[l], w_uq[l], w_ukv[l],
                         mla_q_norm[l], mla_k_norm[l], w_o_na[l], w_o_mla[l], w_out[l],
                         rope_cos, rope_sin)
    return x
```

```python
import contextlib
import os
import numpy as np
import concourse.bass as bass
import concourse.mybir as mybir
from concourse.bass_utils import run_bass_kernel_spmd

F32 = mybir.dt.float32
BF16 = mybir.dt.bfloat16
AF = mybir.ActivationFunctionType
ALU = mybir.AluOpType

D = 1024
DIN = 5024
GW = 64
EPS = 1e-6
NV = 17
TABW = 1408 + 1024 + 768
NEG = -30000.0


class Sem:
    def __init__(self, h):
        self.h = h
        self.total = 0


class Buf:
    __slots__ = ("w", "r", "dsem", "name", "excl")

    def __init__(self, name=""):
        self.excl = False
        self.w = None
        self.r = {}
        self.dsem = None
        self.name = name


class Ctx:
    def __init__(self, nc, es):
        self.nc = nc
        self.es = es
        self.engs = {"pe": nc.tensor, "act": nc.scalar, "dve": nc.vector, "pool": nc.gpsimd, "sp": nc.sync}
        self.esem = {k: Sem(es.enter_context(nc.semaphore("s_" + k))) for k in self.engs}
        self.waited = {k: {} for k in self.engs}
        self.dsems = []
        self.free_dsems = []
        self.bar = Sem(es.enter_context(nc.semaphore("s_bar")))
        self.n_ins = 0

    def _need(self, e, tok):
        if tok is None:
            return
        sem, cnt, eng = tok
        if eng == e:
            return
        target = cnt if eng is not None else sem.total
        w = self.waited[e]
        if w.get(id(sem), 0) >= target:
            return
        self.engs[e].wait_ge(sem.h, target)
        w[id(sem)] = target

    def _deps(self, e, reads, writes):
        for b in reads:
            self._need(e, b.w)
            if b.excl:
                for t in b.r.values():
                    self._need(e, t)
        for b in writes:
            self._need(e, b.w)
            for t in b.r.values():
                self._need(e, t)

    def _mark(self, tok, key, reads, writes):
        for b in reads:
            b.r[key] = tok
        for b in writes:
            b.w = tok
            b.r = {}

    def op(self, e, fn, reads=(), writes=(), sig=True):
        self._deps(e, reads, writes)
        ins = fn(self.engs[e])
        s = self.esem[e]
        if sig:
            s.total += 1
            ins.then_inc(s.h, 1)
            tok = (s, s.total, e)
        else:
            tok = (s, s.total + 1, e)
        self._mark(tok, e, reads, writes)
        self.n_ins += 1
        return ins

    def get_dsem(self, b):
        if b.dsem is None:
            if self.free_dsems:
                b.dsem = self.free_dsems.pop()
            else:
                b.dsem = Sem(self.es.enter_context(self.nc.semaphore("d%d" % len(self.dsems))))
                self.dsems.append(b.dsem)
        return b.dsem

    def release(self, bufs):
        for b in bufs:
            if b.dsem is not None:
                self.free_dsems.append(b.dsem)
                b.dsem = None

    def dma(self, q, out, in_, sb, load):
        if load:
            self._deps(q, (), (sb,))
        else:
            self._deps(q, (sb,), ())
        s = self.get_dsem(sb)
        ins = self.engs[q].dma_start(out=out, in_=in_)
        ins.then_inc(s.h, 16)
        s.total += 16
        tok = (s, s.total, None)
        if load:
            self._mark(tok, id(s), (), (sb,))
        else:
            self._mark(tok, id(s), (sb,), ())
        self.n_ins += 1

    def barrier(self, persistent=()):
        sp = self.engs["sp"]
        w = self.waited["sp"]
        allsems = [s for k, s in self.esem.items() if k != "sp"] + self.dsems
        for s in allsems:
            if w.get(id(s), 0) < s.total:
                sp.wait_ge(s.h, s.total)
                w[id(s)] = s.total
        self.bar.total += 1
        sp.sem_inc(self.bar.h, 1)
        for k in self.engs:
            if k == "sp":
                continue
            self.engs[k].wait_ge(self.bar.h, self.bar.total)
            ww = self.waited[k]
            for s in allsems:
                ww[id(s)] = s.total


_UID = [0]


class _Stop(Exception):
    pass


class Ring:
    def __init__(self, nc, es, name, n, shape, dtype, psum=False):
        self.tiles = []
        self.bufs = []
        for i in range(n):
            _UID[0] += 1
            nm = "%s_%d_%d" % (name, i, _UID[0])
            if psum:
                t = es.enter_context(nc.psum_tensor(nm, shape, dtype))
            else:
                t = es.enter_context(nc.sbuf_tensor(nm, shape, dtype))
            self.tiles.append(t)
            self.bufs.append(Buf("%s%d" % (name, i)))
        self.i = 0

    def next(self):
        t, b = self.tiles[self.i], self.bufs[self.i]
        self.i = (self.i + 1) % len(self.tiles)
        return t, b


def build_program(S, L, NB):
    NCH = S // 512
    NT = S // 128
    ROWS = S // GW
    VW = NT * 65
    nc = bass.Bass("TRN2", target_bir_lowering=False)

    def din(name, shape, dt=F32):
        return nc.dram_tensor(name, list(shape), dt, kind="ExternalInput").ap()

    def dscr(name, shape, dt=BF16):
        return nc.dram_tensor(name, list(shape), dt, kind="Internal").ap()

    xT = din("xT", [NB, D, S])
    w_in = din("w_in", [L, D, DIN])
    w_uq = din("w_uq", [L, 256, 768])
    w_ukv = din("w_ukv", [L, 128, 1024])
    w_o_na = din("w_o_na", [L, 512, D])
    w_o_mla = din("w_o_mla", [L, 512, D])
    w_out = din("w_out", [L, D, D])
    vecs = din("vecs", [L, 128, NV])
    natab = din("natab", [L, 8, 128, TABW])
    ropecs = din("ropecs", [32, 2, S])
    consts = din("consts", [128, 160])
    outT = nc.dram_tensor("outT", [NB, D, S], F32, kind="ExternalOutput").ap()

    xs = [[dscr("xs%d_%d" % (b_, i_), [D, S], F32) for i_ in range(2)] for b_ in range(NB)]
    qna_s = dscr("qna_s", [512, S]); kna_s = dscr("kna_s", [512, S]); gna_s = dscr("gna_s", [512, S])
    vna_s = dscr("vna_s", [8, 128, VW])
    qm_s = dscr("qm_s", [8, 96, S]); km_s = dscr("km_s", [8, 96, S]); gm_s = dscr("gm_s", [512, S])
    vm_s = dscr("vm_s", [8, 128, VW])
    sgn_s = dscr("sgn_s", [D, S]); sgm_s = dscr("sgm_s", [D, S])
    ona_s = dscr("ona_s", [512, S]); om_s = dscr("om_s", [512, S])

    with contextlib.ExitStack() as es:
        es.enter_context(nc.allow_low_precision(reason="bf16 matmul operands, fp32 accumulation"))
        cx = Ctx(nc, es)
        op, dma = cx.op, cx.dma

        cst = es.enter_context(nc.sbuf_tensor("cst", [128, 160], BF16))
        cstb = Buf("cst")
        onesb = es.enter_context(nc.sbuf_tensor("onesb", [128, 128], BF16))
        blk = es.enter_context(nc.sbuf_tensor("blk", [128, 128], BF16))
        onesf = es.enter_context(nc.sbuf_tensor("onesf", [128, 64], F32))
        epsT = es.enter_context(nc.sbuf_tensor("epsT", [128, 1], F32))
        cb = Buf("consts")
        dma("pool", cst[:, :], consts, cstb, True)
        op("dve", lambda e: e.memset(onesb[:, :], 1.0), (), (cb,))
        op("dve", lambda e: e.memset(blk[:, :], 0.0), (), (cb,))
        op("dve", lambda e: e.memset(blk[0:64, 0:64], 1.0), (), (cb,))
        op("dve", lambda e: e.memset(blk[64:128, 64:128], 1.0), (), (cb,))
        op("dve", lambda e: e.memset(onesf[:, :], 1.0), (), (cb,))
        op("dve", lambda e: e.memset(epsT[:, :], EPS), (), (cb,))
        ident = cst[:, 0:128]
        rmat = cst[0:32, 128:160]

        psr = Ring(nc, es, "ps", 8, [128, 512], F32, psum=True)
        for b_ in psr.bufs:
            b_.excl = True

        class SubRing:
            def __init__(self, lo, hi):
                self.tiles = psr.tiles[lo:hi]
                self.bufs = psr.bufs[lo:hi]
                self.i = 0

            def next(self):
                t, b = self.tiles[self.i], self.bufs[self.i]
                self.i = (self.i + 1) % len(self.tiles)
                return t, b

        persist = [cb, cstb] + psr.bufs
        psA = SubRing(0, 2)
        psS = SubRing(2, 6)
        psB = SubRing(6, 8)

        def rstd_from(ps_t, ps_b, M, nq, inv_n, lnr, rsr):
            lt, lb = lnr.next()
            op("act", lambda e: e.activation(out=lt[0:M, 0:nq], in_=ps_t[0:M, 0:nq], func=AF.Ln,
                                             bias=epsT[0:M, 0:1], scale=inv_n), (ps_b, cb), (lb,))
            rt, rb = rsr.next()
            op("act", lambda e: e.activation(out=rt[0:M, 0:nq], in_=lt[0:M, 0:nq], func=AF.Exp, scale=-0.5),
               (lb,), (rb,))
            return rt, rb

        ksub = int(os.environ.get("KSUB", "99"))
        stop_at = int(os.environ.get("KSTOP", "9999"))
        nphase = [0]

        def phase_done():
            nphase[0] += 1

        def active():
            return nphase[0] < stop_at

        try:
          for l in range(L):
              for b in range(NB):
                  xsrc = xT[b] if l == 0 else xs[b][(l - 1) % 2]
                  xdst = outT[b] if l == L - 1 else xs[b][l % 2]
                  xsrc_v = xsrc.rearrange("(k p) t -> p k t", p=128)
                  xdst_v = xdst.rearrange("(k p) t -> p k t", p=128)

                  for half in range(2):
                      if not active():
                          continue
                      c0 = 0 if half == 0 else 2464
                      cw = 2464 if half == 0 else 2560
                      with contextlib.ExitStack() as ps_:
                          pbufs = []

                          def sb(name, shape, dt, n=1):
                              r = Ring(nc, ps_, name, n, shape, dt)
                              pbufs.extend(r.bufs)
                              return r

                          wsb = sb("w_in_sb", [128, 8, cw], BF16)
                          wt, wb = wsb.next()
                          w_in_v = w_in[l].rearrange("(k p) n -> p k n", p=128)
                          for k in range(8):
                              dma("pool", wt[:, k, :], w_in_v[:, k, c0:c0 + cw], wb, True)
                          vr = sb("vec", [128, NV], F32)
                          vt, vb = vr.next()
                          dma("sp", vt[:, :], vecs[l], vb, True)
                          op("dve", lambda e: e.tensor_scalar_mul(out=vt[:, 8:9], in0=vt[:, 8:9], scalar1=0.125), (), (vb,))
                          op("dve", lambda e: e.tensor_scalar_mul(out=vt[:, 13:15], in0=vt[:, 13:15],
                                                                  scalar1=float(96 ** -0.5)), (), (vb,))
                          xr = sb("xt", [128, 8, 512], F32)
                          sqr = sb("sq", [128, 8, 512], BF16)
                          xnr = sb("xn", [128, 8, 512], BF16, 2)
                          lnr = sb("lnt", [128, 512], F32, 2)
                          rsr = sb("rs", [128, 512], F32, 3)
                          rawr = sb("raw", [128, 512], F32, 3)
                          sqhr = sb("sqh", [128, 512], BF16, 3)
                          er = sb("etmp", [128, 512], F32, 3)
                          st4 = sb("st4", [128, 4, 512], BF16, 2)
                          if half == 0:
                              wuq_r = sb("wuq", [128, 2, 768], BF16)
                              wuq, wuqb = wuq_r.next()
                              wuq_v = w_uq[l].rearrange("(k p) n -> p k n", p=128)
                              dma("pool", wuq[:, :, :], wuq_v, wuqb, True)
                              wukv_r = sb("wukv", [128, 1024], BF16)
                              wukv, wukvb = wukv_r.next()
                              dma("pool", wukv[:, :], w_ukv[l], wukvb, True)
                              vstr = sb("vst", [128, 8, 4, 65], BF16, 2)
                              for t_, b_ in zip(vstr.tiles, vstr.bufs):
                                  op("dve", lambda e, t_=t_: e.memset(t_[:, :, :, 64:65], 1.0), (), (b_,))
                              crawr = sb("craw", [128, 2, 512], F32)
                              csqr = sb("csq", [128, 2, 512], BF16)
                              cqnr = sb("cqn", [128, 2, 512], BF16)
                              ckvnr = sb("ckvn", [128, 512], BF16)
                              stn = sb("stn", [64, 512], BF16, 4)
                              stp = sb("stp", [32, 512], BF16, 4)
                              csr = sb("cs", [32, 2, 512], F32)
                              p32r = sb("p32", [32, 512], F32, 4)
                              p16r = sb("p16", [32, 512], BF16, 4)
                              krr = sb("kr", [32, 512], F32)
                              sqper = sb("sqpe", [32, 512], BF16)

                          for c in range(NCH):
                              t0 = c * 512
                              xt, xb = xr.next()
                              dma("sp", xt[:, :, :], xsrc_v[:, :, t0:t0 + 512], xb, True)
                              sq, sqb = sqr.next()
                              op("pool", lambda e: e.tensor_tensor(out=sq[:, :, :], in0=xt[:, :, :], in1=xt[:, :, :],
                                                                   op=ALU.mult), (xb,), (sqb,))
                              pss, pssb = psr.next()
                              for k in range(8):
                                  op("pe", lambda e, k=k: e.matmul(pss[:, :], onesb[:, :], sq[:, k, :],
                                                                   start=(k == 0), stop=(k == 7)),
                                     (sqb, cb), (pssb,), sig=(k == 7))
                              rx, rxb = rstd_from(pss, pssb, 128, 512, 1.0 / D, lnr, rsr)
                              xn, xnb = xnr.next()
                              for k in range(8):
                                  op("dve", lambda e, k=k: e.scalar_tensor_tensor(
                                      out=xn[:, k, :], in0=xt[:, k, :], scalar=vt[:, k:k + 1], in1=rx[:, :],
                                      op0=ALU.mult, op1=ALU.mult), (xb, vb, rxb), (xnb,))

                              def proj_fm(col, M):
                                  pt, pb = psr.next()
                                  for k in range(8):
                                      op("pe", lambda e, k=k: e.matmul(pt[0:M, :], wt[:, k, col - c0:col - c0 + M],
                                                                       xn[:, k, :], start=(k == 0), stop=(k == 7)),
                                         (wb, xnb), (pb,), sig=(k == 7))
                                  return pt, pb

                              def store4(stg, sgb, dst, row0):
                                  dv = dst[row0:row0 + 512, t0:t0 + 512].rearrange("(j p) t -> p j t", p=128)
                                  dma("pool", dv, stg[:, :, :], sgb, False)

                              def gate_chunks(colbase, nchunk, dst, silu):
                                  for g4 in range(nchunk // 4):
                                      stg, sgb = st4.next()
                                      for j in range(4):
                                          jj = g4 * 4 + j
                                          pt, pb = proj_fm(colbase + jj * 128, 128)
                                          et, eb = er.next()
                                          op("act", lambda e: e.activation(out=et[:, :], in_=pt[:, :], func=AF.Exp,
                                                                           scale=-1.0), (pb,), (eb,))
                                          op("dve", lambda e: e.tensor_scalar_add(out=et[:, :], in0=et[:, :], scalar1=1.0),
                                             (), (eb,))
                                          if silu:
                                              op("dve", lambda e: e.reciprocal(out=et[:, :], in_=et[:, :]), (), (eb,))
                                              op("dve", lambda e: e.tensor_tensor(out=stg[:, j, :], in0=pt[:, :],
                                                                                  in1=et[:, :], op=ALU.mult),
                                                 (pb, eb), (sgb,))
                                          else:
                                              op("dve", lambda e: e.reciprocal(out=stg[:, j, :], in_=et[:, :]),
                                                 (eb,), (sgb,))
                                      store4(stg, sgb, dst, g4 * 512)

                              if half == 1:
                                  gate_chunks(2464, 4, gm_s, True)
                                  gate_chunks(2976, 8, sgn_s, False)
                                  gate_chunks(4000, 8, sgm_s, False)
                                  continue

                              if ksub <= 1:
                                  continue
                              for (colbase, gcol, dst) in ((0, 8, qna_s), (512, 9, kna_s)):
                                  stg, sgb = st4.next()
                                  for j in range(4):
                                      pt, pb = proj_fm(colbase + j * 128, 128)
                                      rw, rwb = rawr.next()
                                      op("act", lambda e: e.activation(out=rw[:, :], in_=pt[:, :], func=AF.Copy), (pb,), (rwb,))
                                      sh, shb = sqhr.next()
                                      op("act", lambda e: e.activation(out=sh[:, :], in_=pt[:, :], func=AF.Square), (pb,), (shb,))
                                      p2, p2b = psr.next()
                                      op("pe", lambda e: e.matmul(p2[:, :], blk[:, :], sh[:, :], start=True, stop=True),
                                         (shb, cb), (p2b,))
                                      rs, rsb = rstd_from(p2, p2b, 128, 512, 1.0 / 64, lnr, rsr)
                                      op("dve", lambda e: e.scalar_tensor_tensor(
                                          out=stg[:, j, :], in0=rw[:, :], scalar=vt[:, gcol:gcol + 1], in1=rs[:, :],
                                          op0=ALU.mult, op1=ALU.mult), (rwb, vb, rsb), (sgb,))
                                  store4(stg, sgb, dst, 0)
                              if ksub <= 2:
                                  continue
                              gate_chunks(1536, 4, gna_s, True)
                              if ksub <= 3:
                                  continue
                              vst, vstb = vstr.next()
                              for tt in range(4):
                                  pt, pb = psr.next()
                                  for k in range(8):
                                      op("pe", lambda e, k=k: e.matmul(pt[:, :], xn[:, k, tt * 128:(tt + 1) * 128],
                                                                       wt[:, k, 1024:1536], start=(k == 0), stop=(k == 7)),
                                         (wb, xnb), (pb,), sig=(k == 7))
                                  op("act", lambda e: e.activation(out=vst[:, :, tt, 0:64],
                                                                   in_=pt[:, :].rearrange("p (h e) -> p h e", e=64),
                                                                   func=AF.Copy), (pb,), (vstb,))
                              dma("pool", vna_s.rearrange("h p f -> p h f")[:, :, c * 260:(c + 1) * 260],
                                  vst[:, :, :, :].rearrange("p h t e -> p h (t e)"), vstb, False)

                              if ksub <= 4:
                                  continue
                              cs, csb = csr.next()
                              dma("sp", cs[:, :, :], ropecs[:, :, t0:t0 + 512], csb, True)

                              def rope32(src32, src32b, src16, src16b, out_ap, outb):
                                  pr, prb = psr.next()
                                  op("pe", lambda e: e.matmul(pr[0:32, :], rmat, src16[:, :], start=True, stop=True),
                                     (src16b, cstb), (prb,))
                                  t1, t1b = p32r.next()
                                  op("dve", lambda e: e.tensor_tensor(out=t1[:, :], in0=src32[:, :], in1=cs[:, 0, :],
                                                                      op=ALU.mult), (src32b, csb), (t1b,))
                                  t2, t2b = p32r.next()
                                  op("dve", lambda e: e.tensor_tensor(out=t2[:, :], in0=pr[0:32, :], in1=cs[:, 1, :],
                                                                      op=ALU.mult), (prb, csb), (t2b,))
                                  op("dve", lambda e: e.tensor_tensor(out=out_ap, in0=t1[:, :], in1=t2[:, :], op=ALU.add),
                                     (t1b, t2b), (outb,))

                              craw, crawb = crawr.next()
                              csq, csqb = csqr.next()
                              for j in range(2):
                                  pt, pb = proj_fm(2048 + j * 128, 128)
                                  op("act", lambda e: e.activation(out=craw[:, j, :], in_=pt[:, :], func=AF.Copy), (pb,), (crawb,))
                                  op("act", lambda e: e.activation(out=csq[:, j, :], in_=pt[:, :], func=AF.Square), (pb,), (csqb,))
                              p2, p2b = psr.next()
                              for j in range(2):
                                  op("pe", lambda e, j=j: e.matmul(p2[:, :], onesb[:, :], csq[:, j, :], start=(j == 0), stop=(j == 1)),
                                     (csqb, cb), (p2b,), sig=(j == 1))
                              rs, rsb = rstd_from(p2, p2b, 128, 512, 1.0 / 256, lnr, rsr)
                              cqn, cqnb = cqnr.next()
                              for j in range(2):
                                  op("dve", lambda e, j=j: e.scalar_tensor_tensor(
                                      out=cqn[:, j, :], in0=craw[:, j, :], scalar=vt[:, 10 + j:11 + j], in1=rs[:, :],
                                      op0=ALU.mult, op1=ALU.mult), (crawb, vb, rsb), (cqnb,))

                              def head_norm(pn, pnb, sqp_t, sqp_b):
                                  rw, rwb = rawr.next()
                                  op("act", lambda e: e.activation(out=rw[0:64, :], in_=pn[0:64, :], func=AF.Copy), (pnb,), (rwb,))
                                  sh, shb = sqhr.next()
                                  op("act", lambda e: e.activation(out=sh[0:64, :], in_=pn[0:64, :], func=AF.Square), (pnb,), (shb,))
                                  p3, p3b = psr.next()
                                  op("pe", lambda e: e.matmul(p3[0:64, :], onesb[0:64, 0:64], sh[0:64, :], start=True, stop=False),
                                     (shb, cb), (p3b,), sig=False)
                                  op("pe", lambda e: e.matmul(p3[0:64, :], onesb[0:32, 0:64], sqp_t[:, :], start=False, stop=True),
                                     (sqp_b, cb), (p3b,))
                                  rs_, rsb_ = rstd_from(p3, p3b, 64, 512, 1.0 / 96, lnr, rsr)
                                  return rw, rwb, rs_, rsb_

                              if ksub <= 5:
                                  continue
                              for h in range(8):
                                  pn, pnb = psr.next()
                                  for j in range(2):
                                      op("pe", lambda e, j=j: e.matmul(pn[0:64, :], wuq[:, j, h * 96:h * 96 + 64], cqn[:, j, :],
                                                                       start=(j == 0), stop=(j == 1)),
                                         (wuqb, cqnb), (pnb,), sig=(j == 1))
                                  pp, ppb = psr.next()
                                  for j in range(2):
                                      op("pe", lambda e, j=j: e.matmul(pp[0:32, :], wuq[:, j, h * 96 + 64:h * 96 + 96], cqn[:, j, :],
                                                                       start=(j == 0), stop=(j == 1)),
                                         (wuqb, cqnb), (ppb,), sig=(j == 1))
                                  sqp, sqpb = p16r.next()
                                  op("act", lambda e: e.activation(out=sqp[:, :], in_=pp[0:32, :], func=AF.Square), (ppb,), (sqpb,))
                                  rwp, rwpb = p32r.next()
                                  op("act", lambda e: e.activation(out=rwp[:, :], in_=pp[0:32, :], func=AF.Copy), (ppb,), (rwpb,))
                                  rw, rwb, rs, rsb = head_norm(pn, pnb, sqp, sqpb)
                                  so, sob = stn.next()
                                  op("dve", lambda e: e.scalar_tensor_tensor(
                                      out=so[:, :], in0=rw[0:64, :], scalar=vt[0:64, 13:14], in1=rs[0:64, :],
                                      op0=ALU.mult, op1=ALU.mult), (rwb, vb, rsb), (sob,))
                                  dma("pool", qm_s[h, 0:64, t0:t0 + 512], so[:, :], sob, False)
                                  g32, g32b = p32r.next()
                                  op("dve", lambda e: e.scalar_tensor_tensor(
                                      out=g32[:, :], in0=rwp[:, :], scalar=vt[0:32, 14:15], in1=rs[0:32, :],
                                      op0=ALU.mult, op1=ALU.mult), (rwpb, vb, rsb), (g32b,))
                                  g16, g16b = p16r.next()
                                  op("dve", lambda e: e.tensor_copy(out=g16[:, :], in_=g32[:, :]), (g32b,), (g16b,))
                                  sp_, spb = stp.next()
                                  rope32(g32, g32b, g16, g16b, sp_[:, :], spb)
                                  dma("pool", qm_s[h, 64:96, t0:t0 + 512], sp_[:, :], spb, False)

                              if ksub <= 6:
                                  continue
                              pt, pb = proj_fm(2304, 128)
                              rw, rwb = rawr.next()
                              op("act", lambda e: e.activation(out=rw[:, :], in_=pt[:, :], func=AF.Copy), (pb,), (rwb,))
                              sh, shb = sqhr.next()
                              op("act", lambda e: e.activation(out=sh[:, :], in_=pt[:, :], func=AF.Square), (pb,), (shb,))
                              p2, p2b = psr.next()
                              op("pe", lambda e: e.matmul(p2[:, :], onesb[:, :], sh[:, :], start=True, stop=True), (shb, cb), (p2b,))
                              rs, rsb = rstd_from(p2, p2b, 128, 512, 1.0 / 128, lnr, rsr)
                              ckvn, ckvnb = ckvnr.next()
                              op("dve", lambda e: e.scalar_tensor_tensor(
                                  out=ckvn[:, :], in0=rw[:, :], scalar=vt[:, 12:13], in1=rs[:, :],
                                  op0=ALU.mult, op1=ALU.mult), (rwb, vb, rsb), (ckvnb,))
                              ppe, ppeb = proj_fm(2432, 32)
                              sqpe, sqpeb = sqper.next()
                              op("act", lambda e: e.activation(out=sqpe[:, :], in_=ppe[0:32, :], func=AF.Square), (ppeb,), (sqpeb,))
                              kg32, kg32b = p32r.next()
                              op("dve", lambda e: e.tensor_scalar_mul(out=kg32[:, :], in0=ppe[0:32, :], scalar1=vt[0:32, 16:17]),
                                 (ppeb, vb), (kg32b,))
                              kg16, kg16b = p16r.next()
                              op("dve", lambda e: e.tensor_copy(out=kg16[:, :], in_=kg32[:, :]), (kg32b,), (kg16b,))
                              kr, krb = krr.next()
                              rope32(kg32, kg32b, kg16, kg16b, kr[:, :], krb)
                              if ksub <= 7:
                                  continue
                              for h in range(8):
                                  pn, pnb = psr.next()
                                  op("pe", lambda e: e.matmul(pn[0:64, :], wukv[:, h * 64:(h + 1) * 64], ckvn[:, :],
                                                              start=True, stop=True), (wukvb, ckvnb), (pnb,))
                                  rw, rwb, rs, rsb = head_norm(pn, pnb, sqpe, sqpeb)
                                  so, sob = stn.next()
                                  op("dve", lambda e: e.scalar_tensor_tensor(
                                      out=so[:, :], in0=rw[0:64, :], scalar=vt[0:64, 15:16], in1=rs[0:64, :],
                                      op0=ALU.mult, op1=ALU.mult), (rwb, vb, rsb), (sob,))
                                  dma("pool", km_s[h, 0:64, t0:t0 + 512], so[:, :], sob, False)
                                  sp_, spb = stp.next()
                                  op("dve", lambda e: e.tensor_tensor(out=sp_[:, :], in0=kr[:, :], in1=rs[0:32, :], op=ALU.mult),
                                     (krb, rsb), (spb,))
                                  dma("pool", km_s[h, 64:96, t0:t0 + 512], sp_[:, :], spb, False)
                              if ksub <= 8:
                                  continue
                              vst, vstb = vstr.next()
                              for tt in range(4):
                                  pt, pb = psr.next()
                                  op("pe", lambda e: e.matmul(pt[:, :], ckvn[:, tt * 128:(tt + 1) * 128], wukv[:, 512:1024],
                                                              start=True, stop=True), (wukvb, ckvnb), (pb,))
                                  op("act", lambda e: e.activation(out=vst[:, :, tt, 0:64],
                                                                   in_=pt[:, :].rearrange("p (h e) -> p h e", e=64),
                                                                   func=AF.Copy), (pb,), (vstb,))
                              dma("pool", vm_s.rearrange("h p f -> p h f")[:, :, c * 260:(c + 1) * 260],
                                  vst[:, :, :, :].rearrange("p h t e -> p h (t e)"), vstb, False)

                          cx.barrier(persist)
                          cx.release(pbufs)
                          phase_done()

                  for branch in range(2):
                      if not active():
                          continue
                      mla = (branch == 0)
                      dk = 96 if mla else 64
                      with contextlib.ExitStack() as ps_:
                          pbufs = []

                          def sb(name, shape, dt, n=1):
                              r = Ring(nc, ps_, name, n, shape, dt)
                              pbufs.extend(r.bufs)
                              return r

                          kr_ = sb("K", [dk, S], BF16, 2)
                          vr_ = sb("V", [128, VW], BF16, 2)
                          qr_ = sb("Q", [dk, S], BF16, 2)
                          gr_ = sb("G", [64, S], BF16, 2)
                          ptr = sb("pt", [128, 512], BF16, 4)
                          recr = sb("rec", [65, 512], F32, 2)
                          bcr = sb("bc", [64, 512], F32, 2)
                          obr = sb("ob", [64, 512], BF16, 3)
                          if not mla:
                              tbr = sb("tb", [128, TABW], BF16, 2)

                          for h in range(8):
                              K, Kb = kr_.next()
                              V, Vb = vr_.next()
                              Q, Qb = qr_.next()
                              G, Gb = gr_.next()
                              if mla:
                                  dma("sp", K[:, :], km_s[h], Kb, True)
                                  dma("sp", V[:, :], vm_s[h], Vb, True)
                                  dma("sp", Q[:, :], qm_s[h], Qb, True)
                                  dma("sp", G[:, :], gm_s[h * 64:(h + 1) * 64, :], Gb, True)
                                  dst = om_s
                                  units = [(c * 512, 512, [(j, None) for j in range(NT)]) for c in range(NCH)]
                              else:
                                  dma("sp", K[:, :], kna_s[h * 64:(h + 1) * 64, :], Kb, True)
                                  dma("sp", V[:, :], vna_s[h], Vb, True)
                                  dma("sp", Q[:, :], qna_s[h * 64:(h + 1) * 64, :], Qb, True)
                                  dma("sp", G[:, :], gna_s[h * 64:(h + 1) * 64, :], Gb, True)
                                  tb, tbb = tbr.next()
                                  dma("pool", tb[:, :], natab[l, h], tbb, True)
                                  dst = ona_s
                                  NI = ROWS // 8
                                  units = [(0, 256, [(j, 1408 + j * 256) for j in range(4)])]
                                  for i in range(NI):
                                      qr0 = 4 if i == 0 else 0
                                      qr1 = 5 if i == NI - 1 else 8
                                      keys = []
                                      for j in range(max(0, 4 * i - 2), min(ROWS // 2 - 1, 4 * i + 5) + 1):
                                          dlt = j - 4 * i
                                          keys.append((j, (qr0 - 2 * dlt + 10) * 64))
                                      units.append(((8 * i + qr0) * 64, (qr1 - qr0) * 64, keys))
                                  units.append(((ROWS - 3) * 64, 192,
                                                [(ROWS // 2 - 4 + jj, 1408 + 1024 + jj * 192) for jj in range(4)]))

                              for (q0, nq, keys) in units:
                                  acc, accb = psA.next()
                                  nk = len(keys)
                                  sts = {}

                                  def emit_scores(i):
                                      j, off = keys[i]
                                      st, stb = psS.next()
                                      if off is None:
                                          op("pe", lambda e: e.matmul(st[:, 0:nq], K[:, j * 128:(j + 1) * 128], Q[:, q0:q0 + nq],
                                                                      start=True, stop=True), (Kb, Qb), (stb,))
                                      else:
                                          op("pe", lambda e: e.matmul(st[:, 0:nq], K[:, j * 128:(j + 1) * 128], Q[:, q0:q0 + nq],
                                                                      start=True, stop=False), (Kb, Qb), (stb,), sig=False)
                                          op("pe", lambda e: e.matmul(st[:, 0:nq], ident, tb[:, off:off + nq],
                                                                      start=False, stop=True), (cstb, tbb), (stb,))
                                      sts[i] = (st, stb)

                                  for i in range(min(2, nk)):
                                      emit_scores(i)
                                  for i in range(nk):
                                      st, stb = sts.pop(i)
                                      pt, ptb = ptr.next()
                                      op("act", lambda e: e.activation(out=pt[:, 0:nq], in_=st[:, 0:nq], func=AF.Exp), (stb,), (ptb,))
                                      if i + 2 < nk:
                                          emit_scores(i + 2)
                                      j = keys[i][0]
                                      op("pe", lambda e: e.matmul(acc[0:65, 0:nq], V[:, j * 65:(j + 1) * 65], pt[:, 0:nq],
                                                                  start=(i == 0), stop=(i == nk - 1)), (Vb, ptb), (accb,))
                                  rec, recb = recr.next()
                                  op("dve", lambda e: e.reciprocal(out=rec[64:65, 0:nq], in_=acc[64:65, 0:nq]), (accb,), (recb,))
                                  pbc, pbcb = psB.next()
                                  op("pe", lambda e: e.matmul(pbc[0:64, 0:nq], onesf[64:65, 0:64], rec[64:65, 0:nq],
                                                              start=True, stop=True), (recb, cb), (pbcb,))
                                  bc, bcb = bcr.next()
                                  op("dve", lambda e: e.tensor_tensor(out=bc[:, 0:nq], in0=pbc[0:64, 0:nq], in1=G[:, q0:q0 + nq],
                                                                      op=ALU.mult), (pbcb, Gb), (bcb,))
                                  ob, obb = obr.next()
                                  op("dve", lambda e: e.tensor_tensor(out=ob[:, 0:nq], in0=acc[0:64, 0:nq], in1=bc[:, 0:nq],
                                                                      op=ALU.mult), (accb, bcb), (obb,))
                                  dma("pool", dst[h * 64:(h + 1) * 64, q0:q0 + nq], ob[:, 0:nq], obb, False)

                          cx.barrier(persist)
                          cx.release(pbufs)
                          phase_done()

                  if not active():
                      continue
                  with contextlib.ExitStack() as ps_:
                      pbufs = []

                      def sb(name, shape, dt, n=1):
                          r = Ring(nc, ps_, name, n, shape, dt)
                          pbufs.extend(r.bufs)
                          return r

                      wona, wonab = sb("wona", [64, 8, D], BF16).next()
                      wom, womb = sb("wom", [64, 8, D], BF16).next()
                      wout, woutb = sb("wout", [128, 8, D], BF16).next()
                      dma("pool", wona[:, :, :], w_o_na[l].rearrange("(h d) n -> d h n", d=64), wonab, True)
                      dma("pool", wom[:, :, :], w_o_mla[l].rearrange("(h d) n -> d h n", d=64), womb, True)
                      wout_v = w_out[l].rearrange("(k p) n -> p k n", p=128)
                      for k in range(8):
                          dma("pool", wout[:, k, :], wout_v[:, k, :], woutb, True)
                      onar = sb("ona", [64, 8, 512], BF16, 2)
                      omr = sb("om", [64, 8, 512], BF16, 2)
                      sgnr = sb("sgn", [128, 8, 512], BF16, 2)
                      sgmr = sb("sgm", [128, 8, 512], BF16, 2)
                      xr = sb("xt4", [128, 8, 512], F32, 2)
                      yr = sb("y", [128, 8, 512], BF16, 2)
                      t1r = sb("t1", [128, 512], F32, 3)
                      t2r = sb("t2", [128, 512], F32, 3)
                      for c in range(NCH):
                          t0 = c * 512
                          ona, onab = onar.next()
                          om, omb = omr.next()
                          sgn, sgnb = sgnr.next()
                          sgm, sgmb = sgmr.next()
                          xt, xb = xr.next()
                          dma("sp", ona[:, :, :], ona_s[:, t0:t0 + 512].rearrange("(h d) t -> d h t", d=64), onab, True)
                          dma("sp", om[:, :, :], om_s[:, t0:t0 + 512].rearrange("(h d) t -> d h t", d=64), omb, True)
                          dma("sp", sgn[:, :, :], sgn_s[:, t0:t0 + 512].rearrange("(k p) t -> p k t", p=128), sgnb, True)
                          dma("sp", sgm[:, :, :], sgm_s[:, t0:t0 + 512].rearrange("(k p) t -> p k t", p=128), sgmb, True)
                          dma("sp", xt[:, :, :], xsrc_v[:, :, t0:t0 + 512], xb, True)
                          y, yb = yr.next()
                          for n in range(8):
                              pu1, pu1b = psr.next()
                              for h in range(8):
                                  op("pe", lambda e, h=h: e.matmul(pu1[:, :], wona[:, h, n * 128:(n + 1) * 128], ona[:, h, :],
                                                                   start=(h == 0), stop=(h == 7)), (wonab, onab), (pu1b,), sig=(h == 7))
                              pu2, pu2b = psr.next()
                              for h in range(8):
                                  op("pe", lambda e, h=h: e.matmul(pu2[:, :], wom[:, h, n * 128:(n + 1) * 128], om[:, h, :],
                                                                   start=(h == 0), stop=(h == 7)), (womb, omb), (pu2b,), sig=(h == 7))
                              t1, t1b = t1r.next()
                              op("dve", lambda e: e.tensor_tensor(out=t1[:, :], in0=pu1[:, :], in1=sgn[:, n, :], op=ALU.mult),
                                 (pu1b, sgnb), (t1b,))
                              t2, t2b = t2r.next()
                              op("dve", lambda e: e.tensor_tensor(out=t2[:, :], in0=pu2[:, :], in1=sgm[:, n, :], op=ALU.mult),
                                 (pu2b, sgmb), (t2b,))
                              op("pool", lambda e: e.tensor_tensor(out=y[:, n, :], in0=t1[:, :], in1=t2[:, :], op=ALU.add),
                                 (t1b, t2b), (yb,))
                          for m in range(8):
                              po, pob = psr.next()
                              for n in range(8):
                                  op("pe", lambda e, n=n: e.matmul(po[:, :], wout[:, n, m * 128:(m + 1) * 128], y[:, n, :],
                                                                   start=(n == 0), stop=(n == 7)), (woutb, yb), (pob,), sig=(n == 7))
                              op("dve", lambda e: e.tensor_tensor(out=xt[:, m, :], in0=po[:, :], in1=xt[:, m, :], op=ALU.add),
                                 (pob,), (xb,))
                          dma("pool", xdst_v[:, :, t0:t0 + 512], xt[:, :, :], xb, False)
                      cx.barrier(persist)
                      cx.release(pbufs)
                      phase_done()
        except _Stop:
            pass
        print("instructions emitted:", cx.n_ins)
    return nc


def build_na_tables(rel_bias, ROWS):
    L = rel_bias.shape[0]
    tab = np.full((L, 8, 128, TABW), NEG, dtype=np.float32)
    p = np.arange(128)
    kr2 = p // 64
    kc = p % 64
    c = np.arange(64)
    cs = np.clip(c - 8, 0, GW - 16)
    colvalid = (kc[:, None] >= cs[None, :]) & (kc[:, None] < cs[None, :] + 16)
    dcol = np.clip(kc[:, None] - c[None, :] + 15, 0, 30)
    for a_ in range(22):
        drow = kr2 + 17 - a_
        rowvalid = (drow >= 3) & (drow <= 10)
        valid = rowvalid[:, None] & colvalid
        dr = np.clip(drow, 0, 14)
        vals = rel_bias[:, :, dr[:, None], dcol]
        blk = tab[:, :, :, a_ * 64:(a_ + 1) * 64]
        blk[:, :, valid] = vals[:, :, valid]
    for jj in range(4):
        for qr in range(4):
            drow = 2 * jj + kr2 - qr + 7
            vals = rel_bias[:, :, drow[:, None], dcol]
            o = 1408 + jj * 256 + qr * 64
            blk = tab[:, :, :, o:o + 64]
            blk[:, :, colvalid] = vals[:, :, colvalid]
    for jj in range(4):
        for qq in range(3):
            drow = 2 * jj + kr2 - qq + 2
            vals = rel_bias[:, :, drow[:, None], dcol]
            o = 1408 + 1024 + jj * 192 + qq * 64
            blk = tab[:, :, :, o:o + 64]
            blk[:, :, colvalid] = vals[:, :, colvalid]
    return tab


def rope_const(S):
    t = np.arange(S)
    row = (t // GW).astype(np.float32)
    col = (t % GW).astype(np.float32)
    inv = np.power(np.float32(10000.0), -np.arange(8, dtype=np.float32) / np.float32(8)).astype(np.float32)
    ang = np.concatenate([row[:, None] * inv, col[:, None] * inv], axis=-1).astype(np.float32)
    cos = np.cos(ang.astype(np.float64)).astype(np.float32).T
    sin = np.sin(ang.astype(np.float64)).astype(np.float32).T
    out = np.zeros((32, 2, S), np.float32)
    out[0:16, 0] = cos
    out[16:32, 0] = cos
    out[0:16, 1] = sin
    out[16:32, 1] = sin
    return out


def const_block():
    cst = np.zeros((128, 160), np.float32)
    cst[:, 0:128] = np.eye(128, dtype=np.float32)
    for i in range(16):
        cst[i + 16, 128 + i] = -1.0
        cst[i, 128 + i + 16] = 1.0
    return cst


def pack_shared(inputs, L, S):
    f = lambda a: np.ascontiguousarray(np.asarray(a, dtype=np.float32))
    vec = np.zeros((L, 128, NV), np.float32)
    p = np.arange(128)
    vec[:, :, 0:8] = f(inputs["ln_g"])[:L].reshape(L, 8, 128).transpose(0, 2, 1)
    vec[:, :, 8] = f(inputs["na_q_norm"])[:L][:, p % 64]
    vec[:, :, 9] = f(inputs["na_k_norm"])[:L][:, p % 64]
    vec[:, :, 10:12] = f(inputs["mla_cq_norm"])[:L].reshape(L, 2, 128).transpose(0, 2, 1)
    vec[:, :, 12] = f(inputs["mla_ckv_norm"])[:L]
    vec[:, 0:64, 13] = f(inputs["mla_q_norm"])[:L, 0:64]
    vec[:, 0:32, 14] = f(inputs["mla_q_norm"])[:L, 64:96]
    vec[:, 0:64, 15] = f(inputs["mla_k_norm"])[:L, 0:64]
    vec[:, 0:32, 16] = f(inputs["mla_k_norm"])[:L, 64:96]
    wukv = f(inputs["w_ukv"])[:L].reshape(L, 128, 8, 2, 64)
    wukv = np.ascontiguousarray(wukv.transpose(0, 1, 3, 2, 4)).reshape(L, 128, 1024)
    return {
        "w_in": f(inputs["w_in"])[:L], "w_uq": f(inputs["w_uq"])[:L], "w_ukv": wukv,
        "w_o_na": f(inputs["w_o_na"])[:L], "w_o_mla": f(inputs["w_o_mla"])[:L], "w_out": f(inputs["w_out"])[:L],
        "vecs": vec, "natab": build_na_tables(f(inputs["na_rel_bias"])[:L], S // GW),
        "ropecs": rope_const(S), "consts": const_block(),
    }


_CACHE = {}


def kernel(**inputs):
    x = np.asarray(inputs["x"], dtype=np.float32)
    B, S, _ = x.shape
    L = inputs["ln_g"].shape[0]
    NCORES = 8
    NB = B // NCORES
    key = (S, L, NB)
    if key not in _CACHE:
        _CACHE[key] = build_program(S, L, NB)
    nc = _CACHE[key]
    shared = pack_shared(inputs, L, S)
    in_maps = []
    for ci in range(NCORES):
        m = dict(shared)
        m["xT"] = np.ascontiguousarray(x[ci * NB:(ci + 1) * NB].transpose(0, 2, 1))
        in_maps.append(m)
    res = run_bass_kernel_spmd(nc, in_maps, core_ids=list(range(NCORES)))
    out = np.empty((B, S, D), np.float32)
    for ci in range(NCORES):
        out[ci * NB:(ci + 1) * NB] = np.asarray(res.results[ci]["outT"]).transpose(0, 2, 1)
    return out
```

```python
import contextlib
import os
import numpy as np
import concourse.bass as bass
import concourse.mybir as mybir
from concourse.bass_utils import run_bass_kernel_spmd

F32 = mybir.dt.float32
BF16 = mybir.dt.bfloat16
AF = mybir.ActivationFunctionType
ALU = mybir.AluOpType

D = 1024
DIN = 5024
GW = 64
EPS = 1e-6
NV = 17
TABW = 1408 + 1024 + 768
NEG = -30000.0


class Sem:
    def __init__(self, h):
        self.h = h
        self.total = 0


class Buf:
    __slots__ = ("w", "r", "dsem", "name", "excl")

    def __init__(self, name=""):
        self.excl = False
        self.w = None
        self.r = {}
        self.dsem = None
        self.name = name


class Ctx:
    def __init__(self, nc, es):
        self.nc = nc
        self.es = es
        self.engs = {"pe": nc.tensor, "act": nc.scalar, "dve": nc.vector, "pool": nc.gpsimd, "sp": nc.sync}
        self.esem = {k: Sem(es.enter_context(nc.semaphore("s_" + k))) for k in self.engs}
        self.waited = {k: {} for k in self.engs}
        self.dsems = []
        self.free_dsems = []
        self.bar = Sem(es.enter_context(nc.semaphore("s_bar")))
        self.n_ins = 0

    def _need(self, e, tok):
        if tok is None:
            return
        sem, cnt, eng = tok
        if eng == e:
            return
        target = cnt if eng is not None else sem.total
        w = self.waited[e]
        if w.get(id(sem), 0) >= target:
            return
        self.engs[e].wait_ge(sem.h, target)
        w[id(sem)] = target

    def _deps(self, e, reads, writes):
        for b in reads:
            self._need(e, b.w)
            if b.excl:
                for t in b.r.values():
                    self._need(e, t)
        for b in writes:
            self._need(e, b.w)
            for t in b.r.values():
                self._need(e, t)

    def _mark(self, tok, key, reads, writes):
        for b in reads:
            b.r[key] = tok
        for b in writes:
            b.w = tok
            b.r = {}

    def op(self, e, fn, reads=(), writes=(), sig=True):
        self._deps(e, reads, writes)
        ins = fn(self.engs[e])
        s = self.esem[e]
        if sig:
            s.total += 1
            ins.then_inc(s.h, 1)
            tok = (s, s.total, e)
        else:
            tok = (s, s.total + 1, e)
        self._mark(tok, e, reads, writes)
        self.n_ins += 1
        return ins

    def get_dsem(self, b):
        if b.dsem is None:
            if self.free_dsems:
                b.dsem = self.free_dsems.pop()
            else:
                b.dsem = Sem(self.es.enter_context(self.nc.semaphore("d%d" % len(self.dsems))))
                self.dsems.append(b.dsem)
        return b.dsem

    def release(self, bufs):
        for b in bufs:
            if b.dsem is not None:
                self.free_dsems.append(b.dsem)
                b.dsem = None

    def dma(self, q, out, in_, sb, load):
        if load:
            self._deps(q, (), (sb,))
        else:
            self._deps(q, (sb,), ())
        s = self.get_dsem(sb)
        ins = self.engs[q].dma_start(out=out, in_=in_)
        ins.then_inc(s.h, 16)
        s.total += 16
        tok = (s, s.total, None)
        if load:
            self._mark(tok, id(s), (), (sb,))
        else:
            self._mark(tok, id(s), (sb,), ())
        self.n_ins += 1

    def barrier(self, persistent=()):
        sp = self.engs["sp"]
        w = self.waited["sp"]
        allsems = [s for k, s in self.esem.items() if k != "sp"] + self.dsems
        for s in allsems:
            if w.get(id(s), 0) < s.total:
                sp.wait_ge(s.h, s.total)
                w[id(s)] = s.total
        self.bar.total += 1
        sp.sem_inc(self.bar.h, 1)
        for k in self.engs:
            if k == "sp":
                continue
            self.engs[k].wait_ge(self.bar.h, self.bar.total)
            ww = self.waited[k]
            for s in allsems:
                ww[id(s)] = s.total


_UID = [0]


class _Stop(Exception):
    pass


class Ring:
    def __init__(self, nc, es, name, n, shape, dtype, psum=False):
        self.tiles = []
        self.bufs = []
        for i in range(n):
            _UID[0] += 1
            nm = "%s_%d_%d" % (name, i, _UID[0])
            if psum:
                t = es.enter_context(nc.psum_tensor(nm, shape, dtype))
            else:
                t = es.enter_context(nc.sbuf_tensor(nm, shape, dtype))
            self.tiles.append(t)
            self.bufs.append(Buf("%s%d" % (name, i)))
        self.i = 0

    def next(self):
        t, b = self.tiles[self.i], self.bufs[self.i]
        self.i = (self.i + 1) % len(self.tiles)
        return t, b


def build_program(S, L, NB):
    NCH = S // 512
    NT = S // 128
    ROWS = S // GW
    VW = NT * 65
    nc = bass.Bass("TRN2", target_bir_lowering=False)

    def din(name, shape, dt=F32):
        return nc.dram_tensor(name, list(shape), dt, kind="ExternalInput").ap()

    def dscr(name, shape, dt=BF16):
        return nc.dram_tensor(name, list(shape), dt, kind="Internal").ap()

    xT = din("xT", [NB, D, S])
    w_in = din("w_in", [L, D, DIN])
    w_uq = din("w_uq", [L, 256, 768])
    w_ukv = din("w_ukv", [L, 128, 1024])
    w_o_na = din("w_o_na", [L, 512, D])
    w_o_mla = din("w_o_mla", [L, 512, D])
    w_out = din("w_out", [L, D, D])
    vecs = din("vecs", [L, 128, NV])
    natab = din("natab", [L, 8, 128, TABW])
    ropecs = din("ropecs", [32, 2, S])
    consts = din("consts", [128, 160])
    outT = nc.dram_tensor("outT", [NB, D, S], F32, kind="ExternalOutput").ap()

    xs = [[dscr("xs%d_%d" % (b_, i_), [D, S], F32) for i_ in range(2)] for b_ in range(NB)]
    qna_s = dscr("qna_s", [512, S]); kna_s = dscr("kna_s", [512, S]); gna_s = dscr("gna_s", [512, S])
    vna_s = dscr("vna_s", [8, 128, VW])
    qm_s = dscr("qm_s", [8, 96, S]); km_s = dscr("km_s", [8, 96, S]); gm_s = dscr("gm_s", [512, S])
    vm_s = dscr("vm_s", [8, 128, VW])
    sgn_s = dscr("sgn_s", [D, S]); sgm_s = dscr("sgm_s", [D, S])
    xn_s = dscr("xn_s", [D, S])
    ona_s = dscr("ona_s", [512, S]); om_s = dscr("om_s", [512, S])

    with contextlib.ExitStack() as es:
        es.enter_context(nc.allow_low_precision(reason="bf16 matmul operands, fp32 accumulation"))
        cx = Ctx(nc, es)
        op, dma = cx.op, cx.dma

        cst = es.enter_context(nc.sbuf_tensor("cst", [128, 160], BF16))
        cstb = Buf("cst")
        onesb = es.enter_context(nc.sbuf_tensor("onesb", [128, 128], BF16))
        blk = es.enter_context(nc.sbuf_tensor("blk", [128, 128], BF16))
        onesf = es.enter_context(nc.sbuf_tensor("onesf", [128, 64], F32))
        epsT = es.enter_context(nc.sbuf_tensor("epsT", [128, 1], F32))
        cb = Buf("consts")
        dma("pool", cst[:, :], consts, cstb, True)
        op("dve", lambda e: e.memset(onesb[:, :], 1.0), (), (cb,))
        op("dve", lambda e: e.memset(blk[:, :], 0.0), (), (cb,))
        op("dve", lambda e: e.memset(blk[0:64, 0:64], 1.0), (), (cb,))
        op("dve", lambda e: e.memset(blk[64:128, 64:128], 1.0), (), (cb,))
        op("dve", lambda e: e.memset(onesf[:, :], 1.0), (), (cb,))
        op("dve", lambda e: e.memset(epsT[:, :], EPS), (), (cb,))
        ident = cst[:, 0:128]
        rmat = cst[0:32, 128:160]

        psr = Ring(nc, es, "ps", 8, [128, 512], F32, psum=True)
        for b_ in psr.bufs:
            b_.excl = True

        class SubRing:
            def __init__(self, lo, hi):
                self.tiles = psr.tiles[lo:hi]
                self.bufs = psr.bufs[lo:hi]
                self.i = 0

            def next(self):
                t, b = self.tiles[self.i], self.bufs[self.i]
                self.i = (self.i + 1) % len(self.tiles)
                return t, b

        persist = [cb, cstb] + psr.bufs
        psA = SubRing(0, 2)
        psS = SubRing(2, 6)
        psB = SubRing(6, 8)

        def rstd_from(ps_t, ps_b, M, nq, inv_n, lnr, rsr):
            lt, lb = lnr.next()
            op("act", lambda e: e.activation(out=lt[0:M, 0:nq], in_=ps_t[0:M, 0:nq], func=AF.Ln,
                                             bias=epsT[0:M, 0:1], scale=inv_n), (ps_b, cb), (lb,))
            rt, rb = rsr.next()
            op("act", lambda e: e.activation(out=rt[0:M, 0:nq], in_=lt[0:M, 0:nq], func=AF.Exp, scale=-0.5),
               (lb,), (rb,))
            return rt, rb

        ksub = int(os.environ.get("KSUB", "99"))
        stop_at = int(os.environ.get("KSTOP", "9999"))
        nphase = [0]

        def phase_done():
            nphase[0] += 1

        def active():
            return nphase[0] < stop_at

        try:
          for l in range(L):
              for b in range(NB):
                  xsrc = xT[b] if l == 0 else xs[b][(l - 1) % 2]
                  xdst = outT[b] if l == L - 1 else xs[b][l % 2]
                  xsrc_v = xsrc.rearrange("(k p) t -> p k t", p=128)
                  xdst_v = xdst.rearrange("(k p) t -> p k t", p=128)

                  for half in range(2):
                      if not active():
                          continue
                      if half == 0:
                          segs = [(0, 1536), (2048, 2464)]
                      else:
                          segs = [(1536, 2048), (2464, 5024)]
                      cw = sum(b_ - a_ for a_, b_ in segs)

                      def lcol(col, segs=segs):
                          o = 0
                          for a_, b_ in segs:
                              if a_ <= col < b_:
                                  return o + col - a_
                              o += b_ - a_
                          raise ValueError(col)
                      with contextlib.ExitStack() as ps_:
                          pbufs = []

                          def sb(name, shape, dt, n=1):
                              r = Ring(nc, ps_, name, n, shape, dt)
                              pbufs.extend(r.bufs)
                              return r

                          wsb = sb("w_in_sb", [128, 8, cw], BF16)
                          wt, wb = wsb.next()
                          w_in_v = w_in[l].rearrange("(k p) n -> p k n", p=128)
                          for k in range(8):
                              o_ = 0
                              for a_, b_ in segs:
                                  dma("pool", wt[:, k, o_:o_ + b_ - a_], w_in_v[:, k, a_:b_], wb, True)
                                  o_ += b_ - a_
                          vr = sb("vec", [128, NV], F32)
                          vt, vb = vr.next()
                          dma("sp", vt[:, :], vecs[l], vb, True)
                          op("dve", lambda e: e.tensor_scalar_mul(out=vt[:, 8:9], in0=vt[:, 8:9], scalar1=0.125), (), (vb,))
                          op("dve", lambda e: e.tensor_scalar_mul(out=vt[:, 13:15], in0=vt[:, 13:15],
                                                                  scalar1=float(96 ** -0.5)), (), (vb,))
                          xr = sb("xt", [128, 8, 512], F32)
                          sqr = sb("sq", [128, 8, 512], BF16)
                          xnr = sb("xn", [128, 8, 512], BF16, 2)
                          lnr = sb("lnt", [128, 512], F32, 2)
                          rsr = sb("rs", [128, 512], F32, 3)
                          rawr = sb("raw", [128, 512], F32, 3)
                          sqhr = sb("sqh", [128, 512], BF16, 3)
                          er = sb("etmp", [128, 512], F32, 3)
                          st4 = sb("st4", [128, 4, 512], BF16, 2)
                          if half == 0:
                              wuq_r = sb("wuq", [128, 2, 768], BF16)
                              wuq, wuqb = wuq_r.next()
                              wuq_v = w_uq[l].rearrange("(k p) n -> p k n", p=128)
                              dma("pool", wuq[:, :, :], wuq_v, wuqb, True)
                              wukv_r = sb("wukv", [128, 1024], BF16)
                              wukv, wukvb = wukv_r.next()
                              dma("pool", wukv[:, :], w_ukv[l], wukvb, True)
                              vstr = sb("vst", [128, 8, 4, 65], BF16, 2)
                              for t_, b_ in zip(vstr.tiles, vstr.bufs):
                                  op("dve", lambda e, t_=t_: e.memset(t_[:, :, :, 64:65], 1.0), (), (b_,))
                              crawr = sb("craw", [128, 2, 512], F32)
                              csqr = sb("csq", [128, 2, 512], BF16)
                              cqnr = sb("cqn", [128, 2, 512], BF16)
                              ckvnr = sb("ckvn", [128, 512], BF16)
                              stn = sb("stn", [64, 512], BF16, 4)
                              stp = sb("stp", [32, 512], BF16, 4)
                              csr = sb("cs", [32, 2, 512], F32)
                              p32r = sb("p32", [32, 512], F32, 4)
                              p16r = sb("p16", [32, 512], BF16, 4)
                              krr = sb("kr", [32, 512], F32)
                              sqper = sb("sqpe", [32, 512], BF16)

                          for c in range(NCH):
                              t0 = c * 512
                              if half == 1:
                                  xn, xnb = xnr.next()
                                  dma("sp", xn[:, :, :], xn_s[:, t0:t0 + 512].rearrange("(k p) t -> p k t", p=128), xnb, True)
                              else:
                                  xt, xb = xr.next()
                                  dma("sp", xt[:, :, :], xsrc_v[:, :, t0:t0 + 512], xb, True)
                                  sq, sqb = sqr.next()
                                  op("pool", lambda e: e.tensor_tensor(out=sq[:, :, :], in0=xt[:, :, :], in1=xt[:, :, :],
                                                                       op=ALU.mult), (xb,), (sqb,))
                                  pss, pssb = psr.next()
                                  for k in range(8):
                                      op("pe", lambda e, k=k: e.matmul(pss[:, :], onesb[:, :], sq[:, k, :],
                                                                       start=(k == 0), stop=(k == 7)),
                                         (sqb, cb), (pssb,), sig=(k == 7))
                                  rx, rxb = rstd_from(pss, pssb, 128, 512, 1.0 / D, lnr, rsr)
                                  xn, xnb = xnr.next()
                                  for k in range(8):
                                      op("dve", lambda e, k=k: e.scalar_tensor_tensor(
                                          out=xn[:, k, :], in0=xt[:, k, :], scalar=vt[:, k:k + 1], in1=rx[:, :],
                                          op0=ALU.mult, op1=ALU.mult), (xb, vb, rxb), (xnb,))
                                  dma("pool", xn_s[:, t0:t0 + 512].rearrange("(k p) t -> p k t", p=128), xn[:, :, :], xnb, False)

                              def proj_fm(col, M):
                                  pt, pb = psr.next()
                                  for k in range(8):
                                      op("pe", lambda e, k=k: e.matmul(pt[0:M, :], wt[:, k, lcol(col):lcol(col) + M],
                                                                       xn[:, k, :], start=(k == 0), stop=(k == 7)),
                                         (wb, xnb), (pb,), sig=(k == 7))
                                  return pt, pb

                              def store4(stg, sgb, dst, row0):
                                  dv = dst[row0:row0 + 512, t0:t0 + 512].rearrange("(j p) t -> p j t", p=128)
                                  dma("pool", dv, stg[:, :, :], sgb, False)

                              def gate_chunks(colbase, nchunk, dst, silu):
                                  for g4 in range(nchunk // 4):
                                      stg, sgb = st4.next()
                                      for j in range(4):
                                          jj = g4 * 4 + j
                                          pt, pb = proj_fm(colbase + jj * 128, 128)
                                          if silu:
                                              et, eb = er.next()
                                              op("act", lambda e: e.activation(out=et[:, :], in_=pt[:, :], func=AF.Tanh,
                                                                               scale=0.5), (pb,), (eb,))
                                              op("dve", lambda e: e.scalar_tensor_tensor(
                                                  out=stg[:, j, :], in0=et[:, :], scalar=1.0, in1=pt[:, :],
                                                  op0=ALU.add, op1=ALU.mult), (pb, eb), (sgb,))
                                          else:
                                              op("act", lambda e: e.activation(out=stg[:, j, :], in_=pt[:, :], func=AF.Tanh,
                                                                               scale=0.5), (pb,), (sgb,))
                                      store4(stg, sgb, dst, g4 * 512)

                              if half == 1:
                                  gate_chunks(1536, 4, gna_s, True)
                                  gate_chunks(2464, 4, gm_s, True)
                                  gate_chunks(2976, 8, sgn_s, False)
                                  gate_chunks(4000, 8, sgm_s, False)
                                  continue

                              if ksub <= 1:
                                  continue
                              for (colbase, gcol, dst) in ((0, 8, qna_s), (512, 9, kna_s)):
                                  stg, sgb = st4.next()
                                  for j in range(4):
                                      pt, pb = proj_fm(colbase + j * 128, 128)
                                      rw, rwb = rawr.next()
                                      op("act", lambda e: e.activation(out=rw[:, :], in_=pt[:, :], func=AF.Copy), (pb,), (rwb,))
                                      sh, shb = sqhr.next()
                                      op("act", lambda e: e.activation(out=sh[:, :], in_=pt[:, :], func=AF.Square), (pb,), (shb,))
                                      p2, p2b = psr.next()
                                      op("pe", lambda e: e.matmul(p2[:, :], blk[:, :], sh[:, :], start=True, stop=True),
                                         (shb, cb), (p2b,))
                                      rs, rsb = rstd_from(p2, p2b, 128, 512, 1.0 / 64, lnr, rsr)
                                      op("dve", lambda e: e.scalar_tensor_tensor(
                                          out=stg[:, j, :], in0=rw[:, :], scalar=vt[:, gcol:gcol + 1], in1=rs[:, :],
                                          op0=ALU.mult, op1=ALU.mult), (rwb, vb, rsb), (sgb,))
                                  store4(stg, sgb, dst, 0)
                              if ksub <= 2:
                                  continue
                              if ksub <= 3:
                                  continue
                              vst, vstb = vstr.next()
                              for tt in range(4):
                                  pt, pb = psr.next()
                                  for k in range(8):
                                      op("pe", lambda e, k=k: e.matmul(pt[:, :], xn[:, k, tt * 128:(tt + 1) * 128],
                                                                       wt[:, k, lcol(1024):lcol(1024) + 512], start=(k == 0), stop=(k == 7)),
                                         (wb, xnb), (pb,), sig=(k == 7))
                                  op("act", lambda e: e.activation(out=vst[:, :, tt, 0:64],
                                                                   in_=pt[:, :].rearrange("p (h e) -> p h e", e=64),
                                                                   func=AF.Copy), (pb,), (vstb,))
                              dma("pool", vna_s.rearrange("h p f -> p h f")[:, :, c * 260:(c + 1) * 260],
                                  vst[:, :, :, :].rearrange("p h t e -> p h (t e)"), vstb, False)

                              if ksub <= 4:
                                  continue
                              cs, csb = csr.next()
                              dma("sp", cs[:, :, :], ropecs[:, :, t0:t0 + 512], csb, True)

                              def rope32(src32, src32b, src16, src16b, out_ap, outb):
                                  pr, prb = psr.next()
                                  op("pe", lambda e: e.matmul(pr[0:32, :], rmat, src16[:, :], start=True, stop=True),
                                     (src16b, cstb), (prb,))
                                  t1, t1b = p32r.next()
                                  op("dve", lambda e: e.tensor_tensor(out=t1[:, :], in0=src32[:, :], in1=cs[:, 0, :],
                                                                      op=ALU.mult), (src32b, csb), (t1b,))
                                  t2, t2b = p32r.next()
                                  op("dve", lambda e: e.tensor_tensor(out=t2[:, :], in0=pr[0:32, :], in1=cs[:, 1, :],
                                                                      op=ALU.mult), (prb, csb), (t2b,))
                                  op("dve", lambda e: e.tensor_tensor(out=out_ap, in0=t1[:, :], in1=t2[:, :], op=ALU.add),
                                     (t1b, t2b), (outb,))

                              craw, crawb = crawr.next()
                              csq, csqb = csqr.next()
                              for j in range(2):
                                  pt, pb = proj_fm(2048 + j * 128, 128)
                                  op("act", lambda e: e.activation(out=craw[:, j, :], in_=pt[:, :], func=AF.Copy), (pb,), (crawb,))
                                  op("act", lambda e: e.activation(out=csq[:, j, :], in_=pt[:, :], func=AF.Square), (pb,), (csqb,))
                              p2, p2b = psr.next()
                              for j in range(2):
                                  op("pe", lambda e, j=j: e.matmul(p2[:, :], onesb[:, :], csq[:, j, :], start=(j == 0), stop=(j == 1)),
                                     (csqb, cb), (p2b,), sig=(j == 1))
                              rs, rsb = rstd_from(p2, p2b, 128, 512, 1.0 / 256, lnr, rsr)
                              cqn, cqnb = cqnr.next()
                              for j in range(2):
                                  op("dve", lambda e, j=j: e.scalar_tensor_tensor(
                                      out=cqn[:, j, :], in0=craw[:, j, :], scalar=vt[:, 10 + j:11 + j], in1=rs[:, :],
                                      op0=ALU.mult, op1=ALU.mult), (crawb, vb, rsb), (cqnb,))

                              def head_norm(pn, pnb, sqp_t, sqp_b):
                                  rw, rwb = rawr.next()
                                  op("act", lambda e: e.activation(out=rw[0:64, :], in_=pn[0:64, :], func=AF.Copy), (pnb,), (rwb,))
                                  sh, shb = sqhr.next()
                                  op("act", lambda e: e.activation(out=sh[0:64, :], in_=pn[0:64, :], func=AF.Square), (pnb,), (shb,))
                                  p3, p3b = psr.next()
                                  op("pe", lambda e: e.matmul(p3[0:64, :], onesb[0:64, 0:64], sh[0:64, :], start=True, stop=False),
                                     (shb, cb), (p3b,), sig=False)
                                  op("pe", lambda e: e.matmul(p3[0:64, :], onesb[0:32, 0:64], sqp_t[:, :], start=False, stop=True),
                                     (sqp_b, cb), (p3b,))
                                  rs_, rsb_ = rstd_from(p3, p3b, 64, 512, 1.0 / 96, lnr, rsr)
                                  return rw, rwb, rs_, rsb_

                              if ksub <= 5:
                                  continue
                              for h in range(8):
                                  pn, pnb = psr.next()
                                  for j in range(2):
                                      op("pe", lambda e, j=j: e.matmul(pn[0:64, :], wuq[:, j, h * 96:h * 96 + 64], cqn[:, j, :],
                                                                       start=(j == 0), stop=(j == 1)),
                                         (wuqb, cqnb), (pnb,), sig=(j == 1))
                                  pp, ppb = psr.next()
                                  for j in range(2):
                                      op("pe", lambda e, j=j: e.matmul(pp[0:32, :], wuq[:, j, h * 96 + 64:h * 96 + 96], cqn[:, j, :],
                                                                       start=(j == 0), stop=(j == 1)),
                                         (wuqb, cqnb), (ppb,), sig=(j == 1))
                                  sqp, sqpb = p16r.next()
                                  op("act", lambda e: e.activation(out=sqp[:, :], in_=pp[0:32, :], func=AF.Square), (ppb,), (sqpb,))
                                  rwp, rwpb = p32r.next()
                                  op("act", lambda e: e.activation(out=rwp[:, :], in_=pp[0:32, :], func=AF.Copy), (ppb,), (rwpb,))
                                  rw, rwb, rs, rsb = head_norm(pn, pnb, sqp, sqpb)
                                  so, sob = stn.next()
                                  op("dve", lambda e: e.scalar_tensor_tensor(
                                      out=so[:, :], in0=rw[0:64, :], scalar=vt[0:64, 13:14], in1=rs[0:64, :],
                                      op0=ALU.mult, op1=ALU.mult), (rwb, vb, rsb), (sob,))
                                  dma("pool", qm_s[h, 0:64, t0:t0 + 512], so[:, :], sob, False)
                                  g32, g32b = p32r.next()
                                  op("dve", lambda e: e.scalar_tensor_tensor(
                                      out=g32[:, :], in0=rwp[:, :], scalar=vt[0:32, 14:15], in1=rs[0:32, :],
                                      op0=ALU.mult, op1=ALU.mult), (rwpb, vb, rsb), (g32b,))
                                  g16, g16b = p16r.next()
                                  op("dve", lambda e: e.tensor_copy(out=g16[:, :], in_=g32[:, :]), (g32b,), (g16b,))
                                  sp_, spb = stp.next()
                                  rope32(g32, g32b, g16, g16b, sp_[:, :], spb)
                                  dma("pool", qm_s[h, 64:96, t0:t0 + 512], sp_[:, :], spb, False)

                              if ksub <= 6:
                                  continue
                              pt, pb = proj_fm(2304, 128)
                              rw, rwb = rawr.next()
                              op("act", lambda e: e.activation(out=rw[:, :], in_=pt[:, :], func=AF.Copy), (pb,), (rwb,))
                              sh, shb = sqhr.next()
                              op("act", lambda e: e.activation(out=sh[:, :], in_=pt[:, :], func=AF.Square), (pb,), (shb,))
                              p2, p2b = psr.next()
                              op("pe", lambda e: e.matmul(p2[:, :], onesb[:, :], sh[:, :], start=True, stop=True), (shb, cb), (p2b,))
                              rs, rsb = rstd_from(p2, p2b, 128, 512, 1.0 / 128, lnr, rsr)
                              ckvn, ckvnb = ckvnr.next()
                              op("dve", lambda e: e.scalar_tensor_tensor(
                                  out=ckvn[:, :], in0=rw[:, :], scalar=vt[:, 12:13], in1=rs[:, :],
                                  op0=ALU.mult, op1=ALU.mult), (rwb, vb, rsb), (ckvnb,))
                              ppe, ppeb = proj_fm(2432, 32)
                              sqpe, sqpeb = sqper.next()
                              op("act", lambda e: e.activation(out=sqpe[:, :], in_=ppe[0:32, :], func=AF.Square), (ppeb,), (sqpeb,))
                              kg32, kg32b = p32r.next()
                              op("dve", lambda e: e.tensor_scalar_mul(out=kg32[:, :], in0=ppe[0:32, :], scalar1=vt[0:32, 16:17]),
                                 (ppeb, vb), (kg32b,))
                              kg16, kg16b = p16r.next()
                              op("dve", lambda e: e.tensor_copy(out=kg16[:, :], in_=kg32[:, :]), (kg32b,), (kg16b,))
                              kr, krb = krr.next()
                              rope32(kg32, kg32b, kg16, kg16b, kr[:, :], krb)
                              if ksub <= 7:
                                  continue
                              for h in range(8):
                                  pn, pnb = psr.next()
                                  op("pe", lambda e: e.matmul(pn[0:64, :], wukv[:, h * 64:(h + 1) * 64], ckvn[:, :],
                                                              start=True, stop=True), (wukvb, ckvnb), (pnb,))
                                  rw, rwb, rs, rsb = head_norm(pn, pnb, sqpe, sqpeb)
                                  so, sob = stn.next()
                                  op("dve", lambda e: e.scalar_tensor_tensor(
                                      out=so[:, :], in0=rw[0:64, :], scalar=vt[0:64, 15:16], in1=rs[0:64, :],
                                      op0=ALU.mult, op1=ALU.mult), (rwb, vb, rsb), (sob,))
                                  dma("pool", km_s[h, 0:64, t0:t0 + 512], so[:, :], sob, False)
                                  sp_, spb = stp.next()
                                  op("dve", lambda e: e.tensor_tensor(out=sp_[:, :], in0=kr[:, :], in1=rs[0:32, :], op=ALU.mult),
                                     (krb, rsb), (spb,))
                                  dma("pool", km_s[h, 64:96, t0:t0 + 512], sp_[:, :], spb, False)
                              if ksub <= 8:
                                  continue
                              vst, vstb = vstr.next()
                              for tt in range(4):
                                  pt, pb = psr.next()
                                  op("pe", lambda e: e.matmul(pt[:, :], ckvn[:, tt * 128:(tt + 1) * 128], wukv[:, 512:1024],
                                                              start=True, stop=True), (wukvb, ckvnb), (pb,))
                                  op("act", lambda e: e.activation(out=vst[:, :, tt, 0:64],
                                                                   in_=pt[:, :].rearrange("p (h e) -> p h e", e=64),
                                                                   func=AF.Copy), (pb,), (vstb,))
                              dma("pool", vm_s.rearrange("h p f -> p h f")[:, :, c * 260:(c + 1) * 260],
                                  vst[:, :, :, :].rearrange("p h t e -> p h (t e)"), vstb, False)

                          cx.barrier(persist)
                          cx.release(pbufs)
                          phase_done()

                  for branch in range(2):
                      if not active():
                          continue
                      mla = (branch == 0)
                      dk = 96 if mla else 64
                      with contextlib.ExitStack() as ps_:
                          pbufs = []

                          def sb(name, shape, dt, n=1):
                              r = Ring(nc, ps_, name, n, shape, dt)
                              pbufs.extend(r.bufs)
                              return r

                          kr_ = sb("K", [dk, S], BF16, 2)
                          vr_ = sb("V", [128, VW], BF16, 2)
                          qr_ = sb("Q", [dk, S], BF16, 2)
                          gr_ = sb("G", [64, S], BF16, 2)
                          ptr = sb("pt", [128, 512], BF16, 4)
                          recr = sb("rec", [65, 512], F32, 2)
                          bcr = sb("bc", [64, 512], F32, 2)
                          obr = sb("ob", [64, 512], BF16, 3)
                          if not mla:
                              tbr = sb("tb", [128, TABW], F32, 2)
                              sbr = sb("sbias", [128, 512], F32, 3)

                          for h in range(8):
                              K, Kb = kr_.next()
                              V, Vb = vr_.next()
                              Q, Qb = qr_.next()
                              G, Gb = gr_.next()
                              if mla:
                                  dma("sp", K[:, :], km_s[h], Kb, True)
                                  dma("sp", V[:, :], vm_s[h], Vb, True)
                                  dma("sp", Q[:, :], qm_s[h], Qb, True)
                                  dma("sp", G[:, :], gm_s[h * 64:(h + 1) * 64, :], Gb, True)
                                  dst = om_s
                                  units = [(c * 512, 512, [(j, None) for j in range(NT)]) for c in range(NCH)]
                              else:
                                  dma("sp", K[:, :], kna_s[h * 64:(h + 1) * 64, :], Kb, True)
                                  dma("sp", V[:, :], vna_s[h], Vb, True)
                                  dma("sp", Q[:, :], qna_s[h * 64:(h + 1) * 64, :], Qb, True)
                                  dma("sp", G[:, :], gna_s[h * 64:(h + 1) * 64, :], Gb, True)
                                  tb, tbb = tbr.next()
                                  dma("sp", tb[:, :], natab[l, h], tbb, True)
                                  dst = ona_s
                                  NI = ROWS // 8
                                  units = [(0, 256, [(j, 1408 + j * 256) for j in range(4)])]
                                  for i in range(NI):
                                      qr0 = 4 if i == 0 else 0
                                      qr1 = 5 if i == NI - 1 else 8
                                      keys = []
                                      for j in range(max(0, 4 * i - 2), min(ROWS // 2 - 1, 4 * i + 5) + 1):
                                          dlt = j - 4 * i
                                          keys.append((j, (qr0 - 2 * dlt + 10) * 64))
                                      units.append(((8 * i + qr0) * 64, (qr1 - qr0) * 64, keys))
                                  units.append(((ROWS - 3) * 64, 192,
                                                [(ROWS // 2 - 4 + jj, 1408 + 1024 + jj * 192) for jj in range(4)]))

                              for (q0, nq, keys) in units:
                                  acc, accb = psA.next()
                                  nk = len(keys)
                                  sts = {}

                                  def emit_scores(i):
                                      j, off = keys[i]
                                      st, stb = psS.next()
                                      if off is None:
                                          op("pe", lambda e: e.matmul(st[:, 0:nq], K[:, j * 128:(j + 1) * 128], Q[:, q0:q0 + nq],
                                                                      start=True, stop=True), (Kb, Qb), (stb,))
                                      else:
                                          op("pe", lambda e: e.matmul(st[:, 0:nq], K[:, j * 128:(j + 1) * 128], Q[:, q0:q0 + nq],
                                                                      start=True, stop=True), (Kb, Qb), (stb,))
                                          sbt, sbb = sbr.next()
                                          op("dve", lambda e: e.tensor_tensor(out=sbt[:, 0:nq], in0=st[:, 0:nq], in1=tb[:, off:off + nq],
                                                                              op=ALU.add), (stb, tbb), (sbb,))
                                          st, stb = sbt, sbb
                                      sts[i] = (st, stb)

                                  for i in range(min(2, nk)):
                                      emit_scores(i)
                                  for i in range(nk):
                                      st, stb = sts.pop(i)
                                      pt, ptb = ptr.next()
                                      op("act", lambda e: e.activation(out=pt[:, 0:nq], in_=st[:, 0:nq], func=AF.Exp), (stb,), (ptb,))
                                      if i + 2 < nk:
                                          emit_scores(i + 2)
                                      j = keys[i][0]
                                      op("pe", lambda e: e.matmul(acc[0:65, 0:nq], V[:, j * 65:(j + 1) * 65], pt[:, 0:nq],
                                                                  start=(i == 0), stop=(i == nk - 1)), (Vb, ptb), (accb,))
                                  rec, recb = recr.next()
                                  op("dve", lambda e: e.reciprocal(out=rec[64:65, 0:nq], in_=acc[64:65, 0:nq]), (accb,), (recb,))
                                  pbc, pbcb = psB.next()
                                  op("pe", lambda e: e.matmul(pbc[0:64, 0:nq], onesf[64:65, 0:64], rec[64:65, 0:nq],
                                                              start=True, stop=True), (recb, cb), (pbcb,))
                                  bc, bcb = bcr.next()
                                  op("dve", lambda e: e.scalar_tensor_tensor(out=bc[:, 0:nq], in0=pbc[0:64, 0:nq], scalar=0.5,
                                                                             in1=G[:, q0:q0 + nq], op0=ALU.mult, op1=ALU.mult),
                                     (pbcb, Gb), (bcb,))
                                  ob, obb = obr.next()
                                  op("dve", lambda e: e.tensor_tensor(out=ob[:, 0:nq], in0=acc[0:64, 0:nq], in1=bc[:, 0:nq],
                                                                      op=ALU.mult), (accb, bcb), (obb,))
                                  dma("pool", dst[h * 64:(h + 1) * 64, q0:q0 + nq], ob[:, 0:nq], obb, False)

                          cx.barrier(persist)
                          cx.release(pbufs)
                          phase_done()

                  if not active():
                      continue
                  with contextlib.ExitStack() as ps_:
                      pbufs = []

                      def sb(name, shape, dt, n=1):
                          r = Ring(nc, ps_, name, n, shape, dt)
                          pbufs.extend(r.bufs)
                          return r

                      wona, wonab = sb("wona", [64, 8, D], BF16).next()
                      wom, womb = sb("wom", [64, 8, D], BF16).next()
                      wout, woutb = sb("wout", [128, 8, D], BF16).next()
                      dma("pool", wona[:, :, :], w_o_na[l].rearrange("(h d) n -> d h n", d=64), wonab, True)
                      dma("pool", wom[:, :, :], w_o_mla[l].rearrange("(h d) n -> d h n", d=64), womb, True)
                      wout_v = w_out[l].rearrange("(k p) n -> p k n", p=128)
                      for k in range(8):
                          dma("pool", wout[:, k, :], wout_v[:, k, :], woutb, True)
                      onar = sb("ona", [64, 8, 512], BF16, 2)
                      omr = sb("om", [64, 8, 512], BF16, 2)
                      sgnr = sb("sgn", [128, 8, 512], BF16, 2)
                      sgmr = sb("sgm", [128, 8, 512], BF16, 2)
                      xr = sb("xt4", [128, 8, 512], F32, 2)
                      yr = sb("y", [128, 8, 512], BF16, 2)
                      t1r = sb("t1", [128, 512], F32, 3)
                      t2r = sb("t2", [128, 512], F32, 3)
                      for c in range(NCH):
                          t0 = c * 512
                          ona, onab = onar.next()
                          om, omb = omr.next()
                          sgn, sgnb = sgnr.next()
                          sgm, sgmb = sgmr.next()
                          xt, xb = xr.next()
                          dma("sp", ona[:, :, :], ona_s[:, t0:t0 + 512].rearrange("(h d) t -> d h t", d=64), onab, True)
                          dma("sp", om[:, :, :], om_s[:, t0:t0 + 512].rearrange("(h d) t -> d h t", d=64), omb, True)
                          dma("sp", sgn[:, :, :], sgn_s[:, t0:t0 + 512].rearrange("(k p) t -> p k t", p=128), sgnb, True)
                          dma("sp", sgm[:, :, :], sgm_s[:, t0:t0 + 512].rearrange("(k p) t -> p k t", p=128), sgmb, True)
                          dma("sp", xt[:, :, :], xsrc_v[:, :, t0:t0 + 512], xb, True)
                          y, yb = yr.next()
                          for n in range(8):
                              pu1, pu1b = psr.next()
                              for h in range(8):
                                  op("pe", lambda e, h=h: e.matmul(pu1[:, :], wona[:, h, n * 128:(n + 1) * 128], ona[:, h, :],
                                                                   start=(h == 0), stop=(h == 7)), (wonab, onab), (pu1b,), sig=(h == 7))
                              pu2, pu2b = psr.next()
                              for h in range(8):
                                  op("pe", lambda e, h=h: e.matmul(pu2[:, :], wom[:, h, n * 128:(n + 1) * 128], om[:, h, :],
                                                                   start=(h == 0), stop=(h == 7)), (womb, omb), (pu2b,), sig=(h == 7))
                              t1, t1b = t1r.next()
                              op("dve", lambda e: e.scalar_tensor_tensor(out=t1[:, :], in0=sgn[:, n, :], scalar=1.0, in1=pu1[:, :],
                                                                         op0=ALU.add, op1=ALU.mult), (pu1b, sgnb), (t1b,))
                              t2, t2b = t2r.next()
                              op("dve", lambda e: e.scalar_tensor_tensor(out=t2[:, :], in0=sgm[:, n, :], scalar=1.0, in1=pu2[:, :],
                                                                         op0=ALU.add, op1=ALU.mult), (pu2b, sgmb), (t2b,))
                              op("pool", lambda e: e.tensor_tensor(out=y[:, n, :], in0=t1[:, :], in1=t2[:, :], op=ALU.add),
                                 (t1b, t2b), (yb,))
                          for m in range(8):
                              po, pob = psr.next()
                              for n in range(8):
                                  op("pe", lambda e, n=n: e.matmul(po[:, :], wout[:, n, m * 128:(m + 1) * 128], y[:, n, :],
                                                                   start=(n == 0), stop=(n == 7)), (woutb, yb), (pob,), sig=(n == 7))
                              op("dve", lambda e: e.scalar_tensor_tensor(out=xt[:, m, :], in0=po[:, :], scalar=0.5, in1=xt[:, m, :],
                                                                         op0=ALU.mult, op1=ALU.add), (pob,), (xb,))
                          dma("pool", xdst_v[:, :, t0:t0 + 512], xt[:, :, :], xb, False)
                      cx.barrier(persist)
                      cx.release(pbufs)
                      phase_done()
        except _Stop:
            pass
        print("instructions emitted:", cx.n_ins)
    return nc


def build_na_tables(rel_bias, ROWS):
    L = rel_bias.shape[0]
    tab = np.full((L, 8, 128, TABW), NEG, dtype=np.float32)
    p = np.arange(128)
    kr2 = p // 64
    kc = p % 64
    c = np.arange(64)
    cs = np.clip(c - 8, 0, GW - 16)
    colvalid = (kc[:, None] >= cs[None, :]) & (kc[:, None] < cs[None, :] + 16)
    dcol = np.clip(kc[:, None] - c[None, :] + 15, 0, 30)
    for a_ in range(22):
        drow = kr2 + 17 - a_
        rowvalid = (drow >= 3) & (drow <= 10)
        valid = rowvalid[:, None] & colvalid
        dr = np.clip(drow, 0, 14)
        vals = rel_bias[:, :, dr[:, None], dcol]
        blk = tab[:, :, :, a_ * 64:(a_ + 1) * 64]
        blk[:, :, valid] = vals[:, :, valid]
    for jj in range(4):
        for qr in range(4):
            drow = 2 * jj + kr2 - qr + 7
            vals = rel_bias[:, :, drow[:, None], dcol]
            o = 1408 + jj * 256 + qr * 64
            blk = tab[:, :, :, o:o + 64]
            blk[:, :, colvalid] = vals[:, :, colvalid]
    for jj in range(4):
        for qq in range(3):
            drow = 2 * jj + kr2 - qq + 2
            vals = rel_bias[:, :, drow[:, None], dcol]
            o = 1408 + 1024 + jj * 192 + qq * 64
            blk = tab[:, :, :, o:o + 64]
            blk[:, :, colvalid] = vals[:, :, colvalid]
    return tab


def rope_const(S):
    t = np.arange(S)
    row = (t // GW).astype(np.float32)
    col = (t % GW).astype(np.float32)
    inv = np.power(np.float32(10000.0), -np.arange(8, dtype=np.float32) / np.float32(8)).astype(np.float32)
    ang = np.concatenate([row[:, None] * inv, col[:, None] * inv], axis=-1).astype(np.float32)
    cos = np.cos(ang.astype(np.float64)).astype(np.float32).T
    sin = np.sin(ang.astype(np.float64)).astype(np.float32).T
    out = np.zeros((32, 2, S), np.float32)
    out[0:16, 0] = cos
    out[16:32, 0] = cos
    out[0:16, 1] = sin
    out[16:32, 1] = sin
    return out


def const_block():
    cst = np.zeros((128, 160), np.float32)
    cst[:, 0:128] = np.eye(128, dtype=np.float32)
    for i in range(16):
        cst[i + 16, 128 + i] = -1.0
        cst[i, 128 + i + 16] = 1.0
    return cst


def pack_shared(inputs, L, S):
    f = lambda a: np.ascontiguousarray(np.asarray(a, dtype=np.float32))
    vec = np.zeros((L, 128, NV), np.float32)
    p = np.arange(128)
    vec[:, :, 0:8] = f(inputs["ln_g"])[:L].reshape(L, 8, 128).transpose(0, 2, 1)
    vec[:, :, 8] = f(inputs["na_q_norm"])[:L][:, p % 64]
    vec[:, :, 9] = f(inputs["na_k_norm"])[:L][:, p % 64]
    vec[:, :, 10:12] = f(inputs["mla_cq_norm"])[:L].reshape(L, 2, 128).transpose(0, 2, 1)
    vec[:, :, 12] = f(inputs["mla_ckv_norm"])[:L]
    vec[:, 0:64, 13] = f(inputs["mla_q_norm"])[:L, 0:64]
    vec[:, 0:32, 14] = f(inputs["mla_q_norm"])[:L, 64:96]
    vec[:, 0:64, 15] = f(inputs["mla_k_norm"])[:L, 0:64]
    vec[:, 0:32, 16] = f(inputs["mla_k_norm"])[:L, 64:96]
    wukv = f(inputs["w_ukv"])[:L].reshape(L, 128, 8, 2, 64)
    wukv = np.ascontiguousarray(wukv.transpose(0, 1, 3, 2, 4)).reshape(L, 128, 1024)
    return {
        "w_in": f(inputs["w_in"])[:L], "w_uq": f(inputs["w_uq"])[:L], "w_ukv": wukv,
        "w_o_na": f(inputs["w_o_na"])[:L], "w_o_mla": f(inputs["w_o_mla"])[:L], "w_out": f(inputs["w_out"])[:L],
        "vecs": vec, "natab": build_na_tables(f(inputs["na_rel_bias"])[:L], S // GW),
        "ropecs": rope_const(S), "consts": const_block(),
    }


_CACHE = {}


def kernel(**inputs):
    x = np.asarray(inputs["x"], dtype=np.float32)
    B, S, _ = x.shape
    L = inputs["ln_g"].shape[0]
    NCORES = 8
    NB = B // NCORES
    key = (S, L, NB)
    if key not in _CACHE:
        _CACHE[key] = build_program(S, L, NB)
    nc = _CACHE[key]
    shared = pack_shared(inputs, L, S)
    in_maps = []
    for ci in range(NCORES):
        m = dict(shared)
        m["xT"] = np.ascontiguousarray(x[ci * NB:(ci + 1) * NB].transpose(0, 2, 1))
        in_maps.append(m)
    res = run_bass_kernel_spmd(nc, in_maps, core_ids=list(range(NCORES)))
    out = np.empty((B, S, D), np.float32)
    for ci in range(NCORES):
        out[ci * NB:(ci + 1) * NB] = np.asarray(res.results[ci]["outT"]).transpose(0, 2, 1)
    return out
```

```python
import contextlib
import os
import numpy as np
import concourse.bass as bass
import concourse.mybir as mybir
from concourse.bass_utils import run_bass_kernel_spmd

F32 = mybir.dt.float32
BF16 = mybir.dt.bfloat16
AF = mybir.ActivationFunctionType
ALU = mybir.AluOpType

D = 1024
DIN = 5024
GW = 64
EPS = 1e-6
NV = 17
TABW = 1408 + 1024 + 768
NEG = -30000.0


class Sem:
    def __init__(self, h):
        self.h = h
        self.total = 0


class Buf:
    __slots__ = ("w", "r", "dsem", "name", "excl")

    def __init__(self, name=""):
        self.excl = False
        self.w = None
        self.r = {}
        self.dsem = None
        self.name = name


class Ctx:
    def __init__(self, nc, es):
        self.nc = nc
        self.es = es
        self.engs = {"pe": nc.tensor, "act": nc.scalar, "dve": nc.vector, "pool": nc.gpsimd, "sp": nc.sync}
        self.esem = {k: Sem(es.enter_context(nc.semaphore("s_" + k))) for k in self.engs}
        self.waited = {k: {} for k in self.engs}
        self.dsems = []
        self.free_dsems = []
        self.bar = Sem(es.enter_context(nc.semaphore("s_bar")))
        self.n_ins = 0

    def _need(self, e, tok):
        if tok is None:
            return
        sem, cnt, eng = tok
        if eng == e:
            return
        target = cnt if eng is not None else sem.total
        w = self.waited[e]
        if w.get(id(sem), 0) >= target:
            return
        self.engs[e].wait_ge(sem.h, target)
        w[id(sem)] = target

    def _deps(self, e, reads, writes):
        for b in reads:
            self._need(e, b.w)
            if b.excl:
                for t in b.r.values():
                    self._need(e, t)
        for b in writes:
            self._need(e, b.w)
            for t in b.r.values():
                self._need(e, t)

    def _mark(self, tok, key, reads, writes):
        for b in reads:
            b.r[key] = tok
        for b in writes:
            b.w = tok
            b.r = {}

    def op(self, e, fn, reads=(), writes=(), sig=True):
        self._deps(e, reads, writes)
        ins = fn(self.engs[e])
        s = self.esem[e]
        if sig:
            s.total += 1
            ins.then_inc(s.h, 1)
            tok = (s, s.total, e)
        else:
            tok = (s, s.total + 1, e)
        self._mark(tok, e, reads, writes)
        self.n_ins += 1
        return ins

    def get_dsem(self, b):
        if b.dsem is None:
            if self.free_dsems:
                b.dsem = self.free_dsems.pop()
            else:
                b.dsem = Sem(self.es.enter_context(self.nc.semaphore("d%d" % len(self.dsems))))
                self.dsems.append(b.dsem)
        return b.dsem

    def release(self, bufs):
        for b in bufs:
            if b.dsem is not None:
                self.free_dsems.append(b.dsem)
                b.dsem = None

    def dma(self, q, out, in_, sb, load):
        if load:
            self._deps(q, (), (sb,))
        else:
            self._deps(q, (sb,), ())
        s = self.get_dsem(sb)
        ins = self.engs[q].dma_start(out=out, in_=in_)
        ins.then_inc(s.h, 16)
        s.total += 16
        tok = (s, s.total, None)
        if load:
            self._mark(tok, id(s), (), (sb,))
        else:
            self._mark(tok, id(s), (sb,), ())
        self.n_ins += 1

    def barrier(self, persistent=()):
        sp = self.engs["sp"]
        w = self.waited["sp"]
        allsems = [s for k, s in self.esem.items() if k != "sp"] + self.dsems
        for s in allsems:
            if w.get(id(s), 0) < s.total:
                sp.wait_ge(s.h, s.total)
                w[id(s)] = s.total
        self.bar.total += 1
        sp.sem_inc(self.bar.h, 1)
        for k in self.engs:
            if k == "sp":
                continue
            self.engs[k].wait_ge(self.bar.h, self.bar.total)
            ww = self.waited[k]
            for s in allsems:
                ww[id(s)] = s.total


_UID = [0]


class _Stop(Exception):
    pass


class Ring:
    def __init__(self, nc, es, name, n, shape, dtype, psum=False):
        self.tiles = []
        self.bufs = []
        for i in range(n):
            _UID[0] += 1
            nm = "%s_%d_%d" % (name, i, _UID[0])
            if psum:
                t = es.enter_context(nc.psum_tensor(nm, shape, dtype))
            else:
                t = es.enter_context(nc.sbuf_tensor(nm, shape, dtype))
            self.tiles.append(t)
            self.bufs.append(Buf("%s%d" % (name, i)))
        self.i = 0

    def next(self):
        t, b = self.tiles[self.i], self.bufs[self.i]
        self.i = (self.i + 1) % len(self.tiles)
        return t, b


def build_program(S, L, NB):
    NCH = S // 512
    NT = S // 128
    ROWS = S // GW
    VW = NT * 65
    nc = bass.Bass("TRN2", target_bir_lowering=False)

    def din(name, shape, dt=F32):
        return nc.dram_tensor(name, list(shape), dt, kind="ExternalInput").ap()

    def dscr(name, shape, dt=BF16):
        return nc.dram_tensor(name, list(shape), dt, kind="Internal").ap()

    xT = din("xT", [NB, D, S])
    w_in = din("w_in", [L, D, DIN])
    w_uq = din("w_uq", [L, 256, 768])
    w_ukv = din("w_ukv", [L, 128, 1024])
    w_o_na = din("w_o_na", [L, 512, D])
    w_o_mla = din("w_o_mla", [L, 512, D])
    w_out = din("w_out", [L, D, D])
    vecs = din("vecs", [L, 128, NV])
    natab = din("natab", [L, 8, 128, TABW])
    ropecs = din("ropecs", [32, 2, S])
    consts = din("consts", [128, 160])
    outT = nc.dram_tensor("outT", [NB, D, S], F32, kind="ExternalOutput").ap()

    xs = [[dscr("xs%d_%d" % (b_, i_), [D, S], F32) for i_ in range(2)] for b_ in range(NB)]
    qna_s = dscr("qna_s", [512, S]); kna_s = dscr("kna_s", [512, S]); gna_s = dscr("gna_s", [512, S])
    vna_s = dscr("vna_s", [8, 128, VW])
    qm_s = dscr("qm_s", [8, 96, S]); km_s = dscr("km_s", [8, 96, S]); gm_s = dscr("gm_s", [512, S])
    vm_s = dscr("vm_s", [8, 128, VW])
    sgn_s = dscr("sgn_s", [D, S]); sgm_s = dscr("sgm_s", [D, S])
    xn_s = dscr("xn_s", [D, S])
    ona_s = dscr("ona_s", [512, S]); om_s = dscr("om_s", [512, S])

    with contextlib.ExitStack() as es:
        es.enter_context(nc.allow_low_precision(reason="bf16 matmul operands, fp32 accumulation"))
        cx = Ctx(nc, es)
        op, dma = cx.op, cx.dma

        cst = es.enter_context(nc.sbuf_tensor("cst", [128, 160], BF16))
        cstb = Buf("cst")
        onesb = es.enter_context(nc.sbuf_tensor("onesb", [128, 128], BF16))
        blk = es.enter_context(nc.sbuf_tensor("blk", [128, 128], BF16))
        onesf = es.enter_context(nc.sbuf_tensor("onesf", [128, 64], F32))
        epsT = es.enter_context(nc.sbuf_tensor("epsT", [128, 1], F32))
        cb = Buf("consts")
        dma("pool", cst[:, :], consts, cstb, True)
        op("dve", lambda e: e.memset(onesb[:, :], 1.0), (), (cb,))
        op("dve", lambda e: e.memset(blk[:, :], 0.0), (), (cb,))
        op("dve", lambda e: e.memset(blk[0:64, 0:64], 1.0), (), (cb,))
        op("dve", lambda e: e.memset(blk[64:128, 64:128], 1.0), (), (cb,))
        op("dve", lambda e: e.memset(onesf[:, :], 1.0), (), (cb,))
        op("dve", lambda e: e.memset(epsT[:, :], EPS), (), (cb,))
        ident = cst[:, 0:128]
        rmat = cst[0:32, 128:160]

        psr = Ring(nc, es, "ps", 8, [128, 512], F32, psum=True)
        for b_ in psr.bufs:
            b_.excl = True

        class SubRing:
            def __init__(self, lo, hi):
                self.tiles = psr.tiles[lo:hi]
                self.bufs = psr.bufs[lo:hi]
                self.i = 0

            def next(self):
                t, b = self.tiles[self.i], self.bufs[self.i]
                self.i = (self.i + 1) % len(self.tiles)
                return t, b

        persist = [cb, cstb] + psr.bufs
        psA = SubRing(0, 2)
        psS = SubRing(2, 6)
        psB = SubRing(6, 8)

        def rstd_from(ps_t, ps_b, M, nq, inv_n, lnr, rsr):
            lt, lb = lnr.next()
            op("act", lambda e: e.activation(out=lt[0:M, 0:nq], in_=ps_t[0:M, 0:nq], func=AF.Ln,
                                             bias=epsT[0:M, 0:1], scale=inv_n), (ps_b, cb), (lb,))
            rt, rb = rsr.next()
            op("act", lambda e: e.activation(out=rt[0:M, 0:nq], in_=lt[0:M, 0:nq], func=AF.Exp, scale=-0.5),
               (lb,), (rb,))
            return rt, rb

        ksub = int(os.environ.get("KSUB", "99"))
        stop_at = int(os.environ.get("KSTOP", "9999"))
        nphase = [0]

        def phase_done():
            nphase[0] += 1

        def active():
            return nphase[0] < stop_at

        try:
          for l in range(L):
              for b in range(NB):
                  xsrc = xT[b] if l == 0 else xs[b][(l - 1) % 2]
                  xdst = outT[b] if l == L - 1 else xs[b][l % 2]
                  xsrc_v = xsrc.rearrange("(k p) t -> p k t", p=128)
                  xdst_v = xdst.rearrange("(k p) t -> p k t", p=128)

                  for half in range(2):
                      if not active():
                          continue
                      if half == 0:
                          segs = [(0, 1536), (2048, 2464)]
                      else:
                          segs = [(1536, 2048), (2464, 5024)]
                      cw = sum(b_ - a_ for a_, b_ in segs)

                      def lcol(col, segs=segs):
                          o = 0
                          for a_, b_ in segs:
                              if a_ <= col < b_:
                                  return o + col - a_
                              o += b_ - a_
                          raise ValueError(col)
                      with contextlib.ExitStack() as ps_:
                          pbufs = []

                          def sb(name, shape, dt, n=1):
                              r = Ring(nc, ps_, name, n, shape, dt)
                              pbufs.extend(r.bufs)
                              return r

                          wsb = sb("w_in_sb", [128, 8, cw], BF16)
                          wt, wb = wsb.next()
                          w_in_v = w_in[l].rearrange("(k p) n -> p k n", p=128)
                          for k in range(8):
                              o_ = 0
                              for a_, b_ in segs:
                                  dma("pool", wt[:, k, o_:o_ + b_ - a_], w_in_v[:, k, a_:b_], wb, True)
                                  o_ += b_ - a_
                          vr = sb("vec", [128, NV], F32)
                          vt, vb = vr.next()
                          dma("sp", vt[:, :], vecs[l], vb, True)
                          op("dve", lambda e: e.tensor_scalar_mul(out=vt[:, 8:9], in0=vt[:, 8:9], scalar1=0.125), (), (vb,))
                          op("dve", lambda e: e.tensor_scalar_mul(out=vt[:, 13:15], in0=vt[:, 13:15],
                                                                  scalar1=float(96 ** -0.5)), (), (vb,))
                          xr = sb("xt", [128, 8, 512], F32)
                          sqr = sb("sq", [128, 8, 512], BF16)
                          xnr = sb("xn", [128, 8, 512], BF16, 2)
                          lnr = sb("lnt", [128, 512], F32, 2)
                          rsr = sb("rs", [128, 512], F32, 3)
                          rawr = sb("raw", [128, 512], F32, 3)
                          sqhr = sb("sqh", [128, 512], BF16, 3)
                          er = sb("etmp", [128, 512], F32, 3)
                          st4 = sb("st4", [128, 4, 512], BF16, 2)
                          if half == 0:
                              wuq_r = sb("wuq", [128, 2, 768], BF16)
                              wuq, wuqb = wuq_r.next()
                              wuq_v = w_uq[l].rearrange("(k p) n -> p k n", p=128)
                              dma("pool", wuq[:, :, :], wuq_v, wuqb, True)
                              wukv_r = sb("wukv", [128, 1024], BF16)
                              wukv, wukvb = wukv_r.next()
                              dma("pool", wukv[:, :], w_ukv[l], wukvb, True)
                              vstr = sb("vst", [128, 8, 4, 65], BF16, 2)
                              for t_, b_ in zip(vstr.tiles, vstr.bufs):
                                  op("dve", lambda e, t_=t_: e.memset(t_[:, :, :, 64:65], 1.0), (), (b_,))
                              crawr = sb("craw", [128, 2, 512], F32)
                              csqr = sb("csq", [128, 2, 512], BF16)
                              cqnr = sb("cqn", [128, 2, 512], BF16)
                              ckvnr = sb("ckvn", [128, 512], BF16)
                              stn = sb("stn", [64, 512], BF16, 4)
                              stp = sb("stp", [32, 512], BF16, 4)
                              csr = sb("cs", [32, 2, 512], F32)
                              p32r = sb("p32", [32, 512], F32, 4)
                              p16r = sb("p16", [32, 512], BF16, 4)
                              krr = sb("kr", [32, 512], F32)
                              sqper = sb("sqpe", [32, 512], BF16)

                          for c in range(NCH):
                              t0 = c * 512
                              if half == 1:
                                  xn, xnb = xnr.next()
                                  dma("sp", xn[:, :, :], xn_s[:, t0:t0 + 512].rearrange("(k p) t -> p k t", p=128), xnb, True)
                              else:
                                  xt, xb = xr.next()
                                  dma("sp", xt[:, :, :], xsrc_v[:, :, t0:t0 + 512], xb, True)
                                  sq, sqb = sqr.next()
                                  op("pool", lambda e: e.tensor_tensor(out=sq[:, :, :], in0=xt[:, :, :], in1=xt[:, :, :],
                                                                       op=ALU.mult), (xb,), (sqb,))
                                  pss, pssb = psr.next()
                                  for k in range(8):
                                      op("pe", lambda e, k=k: e.matmul(pss[:, :], onesb[:, :], sq[:, k, :],
                                                                       start=(k == 0), stop=(k == 7)),
                                         (sqb, cb), (pssb,), sig=(k == 7))
                                  rx, rxb = rstd_from(pss, pssb, 128, 512, 1.0 / D, lnr, rsr)
                                  xn, xnb = xnr.next()
                                  for k in range(8):
                                      op("dve", lambda e, k=k: e.scalar_tensor_tensor(
                                          out=xn[:, k, :], in0=xt[:, k, :], scalar=vt[:, k:k + 1], in1=rx[:, :],
                                          op0=ALU.mult, op1=ALU.mult), (xb, vb, rxb), (xnb,))
                                  dma("pool", xn_s[:, t0:t0 + 512].rearrange("(k p) t -> p k t", p=128), xn[:, :, :], xnb, False)

                              def proj_fm(col, M):
                                  pt, pb = psr.next()
                                  for k in range(8):
                                      op("pe", lambda e, k=k: e.matmul(pt[0:M, :], wt[:, k, lcol(col):lcol(col) + M],
                                                                       xn[:, k, :], start=(k == 0), stop=(k == 7)),
                                         (wb, xnb), (pb,), sig=(k == 7))
                                  return pt, pb

                              def store4(stg, sgb, dst, row0):
                                  dv = dst[row0:row0 + 512, t0:t0 + 512].rearrange("(j p) t -> p j t", p=128)
                                  dma("pool", dv, stg[:, :, :], sgb, False)

                              def gate_chunks(colbase, nchunk, dst, silu):
                                  for g4 in range(nchunk // 4):
                                      stg, sgb = st4.next()
                                      for j in range(4):
                                          jj = g4 * 4 + j
                                          pt, pb = proj_fm(colbase + jj * 128, 128)
                                          if silu:
                                              et, eb = er.next()
                                              op("act", lambda e: e.activation(out=et[:, :], in_=pt[:, :], func=AF.Tanh,
                                                                               scale=0.5), (pb,), (eb,))
                                              op("dve", lambda e: e.scalar_tensor_tensor(
                                                  out=stg[:, j, :], in0=et[:, :], scalar=1.0, in1=pt[:, :],
                                                  op0=ALU.add, op1=ALU.mult), (pb, eb), (sgb,))
                                          else:
                                              op("act", lambda e: e.activation(out=stg[:, j, :], in_=pt[:, :], func=AF.Tanh,
                                                                               scale=0.5), (pb,), (sgb,))
                                      store4(stg, sgb, dst, g4 * 512)

                              if half == 1:
                                  gate_chunks(1536, 4, gna_s, True)
                                  gate_chunks(2464, 4, gm_s, True)
                                  gate_chunks(2976, 8, sgn_s, False)
                                  gate_chunks(4000, 8, sgm_s, False)
                                  continue

                              if ksub <= 1:
                                  continue
                              for (colbase, gcol, dst) in ((0, 8, qna_s), (512, 9, kna_s)):
                                  stg, sgb = st4.next()
                                  for j in range(4):
                                      pt, pb = proj_fm(colbase + j * 128, 128)
                                      rw, rwb = rawr.next()
                                      op("act", lambda e: e.activation(out=rw[:, :], in_=pt[:, :], func=AF.Copy), (pb,), (rwb,))
                                      sh, shb = sqhr.next()
                                      op("act", lambda e: e.activation(out=sh[:, :], in_=pt[:, :], func=AF.Square), (pb,), (shb,))
                                      p2, p2b = psr.next()
                                      op("pe", lambda e: e.matmul(p2[:, :], blk[:, :], sh[:, :], start=True, stop=True),
                                         (shb, cb), (p2b,))
                                      rs, rsb = rstd_from(p2, p2b, 128, 512, 1.0 / 64, lnr, rsr)
                                      op("dve", lambda e: e.scalar_tensor_tensor(
                                          out=stg[:, j, :], in0=rw[:, :], scalar=vt[:, gcol:gcol + 1], in1=rs[:, :],
                                          op0=ALU.mult, op1=ALU.mult), (rwb, vb, rsb), (sgb,))
                                  store4(stg, sgb, dst, 0)
                              if ksub <= 2:
                                  continue
                              if ksub <= 3:
                                  continue
                              vst, vstb = vstr.next()
                              for tt in range(4):
                                  pt, pb = psr.next()
                                  for k in range(8):
                                      op("pe", lambda e, k=k: e.matmul(pt[:, :], xn[:, k, tt * 128:(tt + 1) * 128],
                                                                       wt[:, k, lcol(1024):lcol(1024) + 512], start=(k == 0), stop=(k == 7)),
                                         (wb, xnb), (pb,), sig=(k == 7))
                                  op("act", lambda e: e.activation(out=vst[:, :, tt, 0:64],
                                                                   in_=pt[:, :].rearrange("p (h e) -> p h e", e=64),
                                                                   func=AF.Copy), (pb,), (vstb,))
                              dma("pool", vna_s.rearrange("h p f -> p h f")[:, :, c * 260:(c + 1) * 260],
                                  vst[:, :, :, :].rearrange("p h t e -> p h (t e)"), vstb, False)

                              if ksub <= 4:
                                  continue
                              cs, csb = csr.next()
                              dma("sp", cs[:, :, :], ropecs[:, :, t0:t0 + 512], csb, True)

                              def rope32(src32, src32b, src16, src16b, out_ap, outb):
                                  pr, prb = psr.next()
                                  op("pe", lambda e: e.matmul(pr[0:32, :], rmat, src16[:, :], start=True, stop=True),
                                     (src16b, cstb), (prb,))
                                  t1, t1b = p32r.next()
                                  op("dve", lambda e: e.tensor_tensor(out=t1[:, :], in0=src32[:, :], in1=cs[:, 0, :],
                                                                      op=ALU.mult), (src32b, csb), (t1b,))
                                  t2, t2b = p32r.next()
                                  op("dve", lambda e: e.tensor_tensor(out=t2[:, :], in0=pr[0:32, :], in1=cs[:, 1, :],
                                                                      op=ALU.mult), (prb, csb), (t2b,))
                                  op("dve", lambda e: e.tensor_tensor(out=out_ap, in0=t1[:, :], in1=t2[:, :], op=ALU.add),
                                     (t1b, t2b), (outb,))

                              craw, crawb = crawr.next()
                              csq, csqb = csqr.next()
                              for j in range(2):
                                  pt, pb = proj_fm(2048 + j * 128, 128)
                                  op("act", lambda e: e.activation(out=craw[:, j, :], in_=pt[:, :], func=AF.Copy), (pb,), (crawb,))
                                  op("act", lambda e: e.activation(out=csq[:, j, :], in_=pt[:, :], func=AF.Square), (pb,), (csqb,))
                              p2, p2b = psr.next()
                              for j in range(2):
                                  op("pe", lambda e, j=j: e.matmul(p2[:, :], onesb[:, :], csq[:, j, :], start=(j == 0), stop=(j == 1)),
                                     (csqb, cb), (p2b,), sig=(j == 1))
                              rs, rsb = rstd_from(p2, p2b, 128, 512, 1.0 / 256, lnr, rsr)
                              cqn, cqnb = cqnr.next()
                              for j in range(2):
                                  op("dve", lambda e, j=j: e.scalar_tensor_tensor(
                                      out=cqn[:, j, :], in0=craw[:, j, :], scalar=vt[:, 10 + j:11 + j], in1=rs[:, :],
                                      op0=ALU.mult, op1=ALU.mult), (crawb, vb, rsb), (cqnb,))

                              def head_norm(pn, pnb, sqp_t, sqp_b):
                                  rw, rwb = rawr.next()
                                  op("act", lambda e: e.activation(out=rw[0:64, :], in_=pn[0:64, :], func=AF.Copy), (pnb,), (rwb,))
                                  sh, shb = sqhr.next()
                                  op("act", lambda e: e.activation(out=sh[0:64, :], in_=pn[0:64, :], func=AF.Square), (pnb,), (shb,))
                                  p3, p3b = psr.next()
                                  op("pe", lambda e: e.matmul(p3[0:64, :], onesb[0:64, 0:64], sh[0:64, :], start=True, stop=False),
                                     (shb, cb), (p3b,), sig=False)
                                  op("pe", lambda e: e.matmul(p3[0:64, :], onesb[0:32, 0:64], sqp_t[:, :], start=False, stop=True),
                                     (sqp_b, cb), (p3b,))
                                  rs_, rsb_ = rstd_from(p3, p3b, 64, 512, 1.0 / 96, lnr, rsr)
                                  return rw, rwb, rs_, rsb_

                              if ksub <= 5:
                                  continue
                              for h in range(8):
                                  pn, pnb = psr.next()
                                  for j in range(2):
                                      op("pe", lambda e, j=j: e.matmul(pn[0:64, :], wuq[:, j, h * 96:h * 96 + 64], cqn[:, j, :],
                                                                       start=(j == 0), stop=(j == 1)),
                                         (wuqb, cqnb), (pnb,), sig=(j == 1))
                                  pp, ppb = psr.next()
                                  for j in range(2):
                                      op("pe", lambda e, j=j: e.matmul(pp[0:32, :], wuq[:, j, h * 96 + 64:h * 96 + 96], cqn[:, j, :],
                                                                       start=(j == 0), stop=(j == 1)),
                                         (wuqb, cqnb), (ppb,), sig=(j == 1))
                                  sqp, sqpb = p16r.next()
                                  op("act", lambda e: e.activation(out=sqp[:, :], in_=pp[0:32, :], func=AF.Square), (ppb,), (sqpb,))
                                  rwp, rwpb = p32r.next()
                                  op("act", lambda e: e.activation(out=rwp[:, :], in_=pp[0:32, :], func=AF.Copy), (ppb,), (rwpb,))
                                  rw, rwb, rs, rsb = head_norm(pn, pnb, sqp, sqpb)
                                  so, sob = stn.next()
                                  op("dve", lambda e: e.scalar_tensor_tensor(
                                      out=so[:, :], in0=rw[0:64, :], scalar=vt[0:64, 13:14], in1=rs[0:64, :],
                                      op0=ALU.mult, op1=ALU.mult), (rwb, vb, rsb), (sob,))
                                  dma("pool", qm_s[h, 0:64, t0:t0 + 512], so[:, :], sob, False)
                                  g32, g32b = p32r.next()
                                  op("dve", lambda e: e.scalar_tensor_tensor(
                                      out=g32[:, :], in0=rwp[:, :], scalar=vt[0:32, 14:15], in1=rs[0:32, :],
                                      op0=ALU.mult, op1=ALU.mult), (rwpb, vb, rsb), (g32b,))
                                  g16, g16b = p16r.next()
                                  op("dve", lambda e: e.tensor_copy(out=g16[:, :], in_=g32[:, :]), (g32b,), (g16b,))
                                  sp_, spb = stp.next()
                                  rope32(g32, g32b, g16, g16b, sp_[:, :], spb)
                                  dma("pool", qm_s[h, 64:96, t0:t0 + 512], sp_[:, :], spb, False)

                              if ksub <= 6:
                                  continue
                              pt, pb = proj_fm(2304, 128)
                              rw, rwb = rawr.next()
                              op("act", lambda e: e.activation(out=rw[:, :], in_=pt[:, :], func=AF.Copy), (pb,), (rwb,))
                              sh, shb = sqhr.next()
                              op("act", lambda e: e.activation(out=sh[:, :], in_=pt[:, :], func=AF.Square), (pb,), (shb,))
                              p2, p2b = psr.next()
                              op("pe", lambda e: e.matmul(p2[:, :], onesb[:, :], sh[:, :], start=True, stop=True), (shb, cb), (p2b,))
                              rs, rsb = rstd_from(p2, p2b, 128, 512, 1.0 / 128, lnr, rsr)
                              ckvn, ckvnb = ckvnr.next()
                              op("dve", lambda e: e.scalar_tensor_tensor(
                                  out=ckvn[:, :], in0=rw[:, :], scalar=vt[:, 12:13], in1=rs[:, :],
                                  op0=ALU.mult, op1=ALU.mult), (rwb, vb, rsb), (ckvnb,))
                              ppe, ppeb = proj_fm(2432, 32)
                              sqpe, sqpeb = sqper.next()
                              op("act", lambda e: e.activation(out=sqpe[:, :], in_=ppe[0:32, :], func=AF.Square), (ppeb,), (sqpeb,))
                              kg32, kg32b = p32r.next()
                              op("dve", lambda e: e.tensor_scalar_mul(out=kg32[:, :], in0=ppe[0:32, :], scalar1=vt[0:32, 16:17]),
                                 (ppeb, vb), (kg32b,))
                              kg16, kg16b = p16r.next()
                              op("dve", lambda e: e.tensor_copy(out=kg16[:, :], in_=kg32[:, :]), (kg32b,), (kg16b,))
                              kr, krb = krr.next()
                              rope32(kg32, kg32b, kg16, kg16b, kr[:, :], krb)
                              if ksub <= 7:
                                  continue
                              for h in range(8):
                                  pn, pnb = psr.next()
                                  op("pe", lambda e: e.matmul(pn[0:64, :], wukv[:, h * 64:(h + 1) * 64], ckvn[:, :],
                                                              start=True, stop=True), (wukvb, ckvnb), (pnb,))
                                  rw, rwb, rs, rsb = head_norm(pn, pnb, sqpe, sqpeb)
                                  so, sob = stn.next()
                                  op("dve", lambda e: e.scalar_tensor_tensor(
                                      out=so[:, :], in0=rw[0:64, :], scalar=vt[0:64, 15:16], in1=rs[0:64, :],
                                      op0=ALU.mult, op1=ALU.mult), (rwb, vb, rsb), (sob,))
                                  dma("pool", km_s[h, 0:64, t0:t0 + 512], so[:, :], sob, False)
                                  sp_, spb = stp.next()
                                  op("dve", lambda e: e.tensor_tensor(out=sp_[:, :], in0=kr[:, :], in1=rs[0:32, :], op=ALU.mult),
                                     (krb, rsb), (spb,))
                                  dma("pool", km_s[h, 64:96, t0:t0 + 512], sp_[:, :], spb, False)
                              if ksub <= 8:
                                  continue
                              vst, vstb = vstr.next()
                              for tt in range(4):
                                  pt, pb = psr.next()
                                  op("pe", lambda e: e.matmul(pt[:, :], ckvn[:, tt * 128:(tt + 1) * 128], wukv[:, 512:1024],
                                                              start=True, stop=True), (wukvb, ckvnb), (pb,))
                                  op("act", lambda e: e.activation(out=vst[:, :, tt, 0:64],
                                                                   in_=pt[:, :].rearrange("p (h e) -> p h e", e=64),
                                                                   func=AF.Copy), (pb,), (vstb,))
                              dma("pool", vm_s.rearrange("h p f -> p h f")[:, :, c * 260:(c + 1) * 260],
                                  vst[:, :, :, :].rearrange("p h t e -> p h (t e)"), vstb, False)

                          cx.barrier(persist)
                          cx.release(pbufs)
                          phase_done()

                  if active():
                      with contextlib.ExitStack() as ps_:
                          pbufs = []

                          def sb(name, shape, dt, n=1):
                              r = Ring(nc, ps_, name, n, shape, dt)
                              pbufs.extend(r.bufs)
                              return r

                          NI = ROWS // 8
                          R = {}
                          for nm, dk in (("m", 96), ("n", 64)):
                              R[nm] = dict(K=sb("K" + nm, [dk, S], BF16, 2), V=sb("V" + nm, [128, VW], BF16, 2),
                                           Q=sb("Q" + nm, [dk, S], BF16, 2), G=sb("G" + nm, [64, S], BF16, 2),
                                           pt=sb("pt" + nm, [128, 512], BF16, 4), rec=sb("rec" + nm, [65, 512], F32, 2),
                                           bc=sb("bc" + nm, [64, 512], F32, 2), ob=sb("ob" + nm, [64, 512], BF16, 3))
                          R["m"].update(acc=SubRing(0, 2), st=SubRing(3, 6), depth=2)
                          R["n"].update(acc=SubRing(2, 3), st=SubRing(6, 7), depth=1)
                          bcps = SubRing(7, 8)
                          tbr = sb("tb", [128, TABW], F32, 2)
                          sbr = sb("sbias", [128, 512], F32, 3)

                          def unit_gen(rr, K, Kb, V, Vb, Q, Qb, G, Gb, tb, tbb, q0, nq, keys, h, dst):
                              acc, accb = rr["acc"].next()
                              nk = len(keys)
                              d = rr["depth"]
                              sts = {}

                              def emit_scores(i):
                                  j, off = keys[i]
                                  st, stb = rr["st"].next()
                                  op("pe", lambda e: e.matmul(st[:, 0:nq], K[:, j * 128:(j + 1) * 128], Q[:, q0:q0 + nq],
                                                              start=True, stop=True), (Kb, Qb), (stb,))
                                  if off is not None:
                                      sbt, sbb = sbr.next()
                                      op("dve", lambda e: e.tensor_tensor(out=sbt[:, 0:nq], in0=st[:, 0:nq], in1=tb[:, off:off + nq],
                                                                          op=ALU.add), (stb, tbb), (sbb,))
                                      st, stb = sbt, sbb
                                  sts[i] = (st, stb)

                              for i in range(min(d, nk)):
                                  emit_scores(i)
                              pend = None
                              for i in range(nk):
                                  if i not in sts:
                                      emit_scores(i)
                                  st, stb = sts.pop(i)
                                  pt, ptb = rr["pt"].next()
                                  op("act", lambda e: e.activation(out=pt[:, 0:nq], in_=st[:, 0:nq], func=AF.Exp), (stb,), (ptb,))
                                  if d > 0 and i + d < nk:
                                      emit_scores(i + d)
                                  if pend is not None:
                                      pi, ppt, pptb = pend
                                      pj = keys[pi][0]
                                      op("pe", lambda e: e.matmul(acc[0:65, 0:nq], V[:, pj * 65:(pj + 1) * 65], ppt[:, 0:nq],
                                                                  start=(pi == 0), stop=False), (Vb, pptb), (accb,), sig=False)
                                  pend = (i, pt, ptb)
                                  yield
                              pi, ppt, pptb = pend
                              pj = keys[pi][0]
                              op("pe", lambda e: e.matmul(acc[0:65, 0:nq], V[:, pj * 65:(pj + 1) * 65], ppt[:, 0:nq],
                                                          start=(pi == 0), stop=True), (Vb, pptb), (accb,))
                              rec, recb = rr["rec"].next()
                              op("dve", lambda e: e.reciprocal(out=rec[64:65, 0:nq], in_=acc[64:65, 0:nq]), (accb,), (recb,))
                              pbc, pbcb = bcps.next()
                              op("pe", lambda e: e.matmul(pbc[0:64, 0:nq], onesf[64:65, 0:64], rec[64:65, 0:nq],
                                                          start=True, stop=True), (recb, cb), (pbcb,))
                              bc, bcb = rr["bc"].next()
                              op("dve", lambda e: e.scalar_tensor_tensor(out=bc[:, 0:nq], in0=pbc[0:64, 0:nq], scalar=0.5,
                                                                         in1=G[:, q0:q0 + nq], op0=ALU.mult, op1=ALU.mult),
                                 (pbcb, Gb), (bcb,))
                              ob, obb = rr["ob"].next()
                              op("dve", lambda e: e.tensor_tensor(out=ob[:, 0:nq], in0=acc[0:64, 0:nq], in1=bc[:, 0:nq],
                                                                  op=ALU.mult), (accb, bcb), (obb,))
                              dma("pool", dst[h * 64:(h + 1) * 64, q0:q0 + nq], ob[:, 0:nq], obb, False)
                              yield

                          def head_gen(nm, h):
                              rr = R[nm]
                              K, Kb = rr["K"].next()
                              V, Vb = rr["V"].next()
                              Q, Qb = rr["Q"].next()
                              G, Gb = rr["G"].next()
                              if nm == "m":
                                  dma("sp", K[:, :], km_s[h], Kb, True)
                                  dma("sp", V[:, :], vm_s[h], Vb, True)
                                  dma("sp", Q[:, :], qm_s[h], Qb, True)
                                  dma("sp", G[:, :], gm_s[h * 64:(h + 1) * 64, :], Gb, True)
                                  tb, tbb = None, None
                                  dst = om_s
                                  units = [(c * 512, 512, [(j, None) for j in range(NT)]) for c in range(NCH)]
                              else:
                                  dma("sp", K[:, :], kna_s[h * 64:(h + 1) * 64, :], Kb, True)
                                  dma("sp", V[:, :], vna_s[h], Vb, True)
                                  dma("sp", Q[:, :], qna_s[h * 64:(h + 1) * 64, :], Qb, True)
                                  dma("sp", G[:, :], gna_s[h * 64:(h + 1) * 64, :], Gb, True)
                                  tb, tbb = tbr.next()
                                  dma("sp", tb[:, :], natab[l, h], tbb, True)
                                  dst = ona_s
                                  units = [(0, 256, [(j, 1408 + j * 256) for j in range(4)])]
                                  for i in range(NI):
                                      qr0 = 4 if i == 0 else 0
                                      qr1 = 5 if i == NI - 1 else 8
                                      keys = []
                                      for j in range(max(0, 4 * i - 2), min(ROWS // 2 - 1, 4 * i + 5) + 1):
                                          dlt = j - 4 * i
                                          keys.append((j, (qr0 - 2 * dlt + 10) * 64))
                                      units.append(((8 * i + qr0) * 64, (qr1 - qr0) * 64, keys))
                                  units.append(((ROWS - 3) * 64, 192,
                                                [(ROWS // 2 - 4 + jj, 1408 + 1024 + jj * 192) for jj in range(4)]))
                              for (q0, nq, keys) in units:
                                  for _ in unit_gen(rr, K, Kb, V, Vb, Q, Qb, G, Gb, tb, tbb, q0, nq, keys, h, dst):
                                      yield

                          n_m = NCH * (NT + 1)
                          n_n = sum(1 for _ in [0]) * 0 + (4 + 1) * 2 + sum(
                              (min(ROWS // 2 - 1, 4 * i + 5) - max(0, 4 * i - 2) + 1) + 1 for i in range(NI))
                          ratio = max(1, int(round(n_m / float(n_n))))
                          for h in range(8):
                              gm = head_gen("m", h)
                              gn = head_gen("n", h)
                              alive_m = alive_n = True
                              while alive_m or alive_n:
                                  for _ in range(ratio):
                                      if alive_m and next(gm, "end") == "end":
                                          alive_m = False
                                  if alive_n and next(gn, "end") == "end":
                                      alive_n = False

                          cx.barrier(persist)
                          cx.release(pbufs)
                          phase_done()

                  if not active():
                      continue
                  with contextlib.ExitStack() as ps_:
                      pbufs = []

                      def sb(name, shape, dt, n=1):
                          r = Ring(nc, ps_, name, n, shape, dt)
                          pbufs.extend(r.bufs)
                          return r

                      wona, wonab = sb("wona", [64, 8, D], BF16).next()
                      wom, womb = sb("wom", [64, 8, D], BF16).next()
                      wout, woutb = sb("wout", [128, 8, D], BF16).next()
                      dma("pool", wona[:, :, :], w_o_na[l].rearrange("(h d) n -> d h n", d=64), wonab, True)
                      dma("pool", wom[:, :, :], w_o_mla[l].rearrange("(h d) n -> d h n", d=64), womb, True)
                      wout_v = w_out[l].rearrange("(k p) n -> p k n", p=128)
                      for k in range(8):
                          dma("pool", wout[:, k, :], wout_v[:, k, :], woutb, True)
                      onar = sb("ona", [64, 8, 512], BF16, 2)
                      omr = sb("om", [64, 8, 512], BF16, 2)
                      sgnr = sb("sgn", [128, 8, 512], BF16, 2)
                      sgmr = sb("sgm", [128, 8, 512], BF16, 2)
                      xr = sb("xt4", [128, 8, 512], F32, 2)
                      yr = sb("y", [128, 8, 512], BF16, 2)
                      t1r = sb("t1", [128, 512], F32, 3)
                      t2r = sb("t2", [128, 512], F32, 3)
                      for c in range(NCH):
                          t0 = c * 512
                          ona, onab = onar.next()
                          om, omb = omr.next()
                          sgn, sgnb = sgnr.next()
                          sgm, sgmb = sgmr.next()
                          xt, xb = xr.next()
                          dma("sp", ona[:, :, :], ona_s[:, t0:t0 + 512].rearrange("(h d) t -> d h t", d=64), onab, True)
                          dma("sp", om[:, :, :], om_s[:, t0:t0 + 512].rearrange("(h d) t -> d h t", d=64), omb, True)
                          dma("sp", sgn[:, :, :], sgn_s[:, t0:t0 + 512].rearrange("(k p) t -> p k t", p=128), sgnb, True)
                          dma("sp", sgm[:, :, :], sgm_s[:, t0:t0 + 512].rearrange("(k p) t -> p k t", p=128), sgmb, True)
                          dma("sp", xt[:, :, :], xsrc_v[:, :, t0:t0 + 512], xb, True)
                          y, yb = yr.next()
                          for n in range(8):
                              pu1, pu1b = psr.next()
                              for h in range(8):
                                  op("pe", lambda e, h=h: e.matmul(pu1[:, :], wona[:, h, n * 128:(n + 1) * 128], ona[:, h, :],
                                                                   start=(h == 0), stop=(h == 7)), (wonab, onab), (pu1b,), sig=(h == 7))
                              pu2, pu2b = psr.next()
                              for h in range(8):
                                  op("pe", lambda e, h=h: e.matmul(pu2[:, :], wom[:, h, n * 128:(n + 1) * 128], om[:, h, :],
                                                                   start=(h == 0), stop=(h == 7)), (womb, omb), (pu2b,), sig=(h == 7))
                              t1, t1b = t1r.next()
                              op("dve", lambda e: e.scalar_tensor_tensor(out=t1[:, :], in0=sgn[:, n, :], scalar=1.0, in1=pu1[:, :],
                                                                         op0=ALU.add, op1=ALU.mult), (pu1b, sgnb), (t1b,))
                              t2, t2b = t2r.next()
                              op("dve", lambda e: e.scalar_tensor_tensor(out=t2[:, :], in0=sgm[:, n, :], scalar=1.0, in1=pu2[:, :],
                                                                         op0=ALU.add, op1=ALU.mult), (pu2b, sgmb), (t2b,))
                              op("pool", lambda e: e.tensor_tensor(out=y[:, n, :], in0=t1[:, :], in1=t2[:, :], op=ALU.add),
                                 (t1b, t2b), (yb,))
                          for m in range(8):
                              po, pob = psr.next()
                              for n in range(8):
                                  op("pe", lambda e, n=n: e.matmul(po[:, :], wout[:, n, m * 128:(m + 1) * 128], y[:, n, :],
                                                                   start=(n == 0), stop=(n == 7)), (woutb, yb), (pob,), sig=(n == 7))
                              op("dve", lambda e: e.scalar_tensor_tensor(out=xt[:, m, :], in0=po[:, :], scalar=0.5, in1=xt[:, m, :],
                                                                         op0=ALU.mult, op1=ALU.add), (pob,), (xb,))
                          dma("pool", xdst_v[:, :, t0:t0 + 512], xt[:, :, :], xb, False)
                      cx.barrier(persist)
                      cx.release(pbufs)
                      phase_done()
        except _Stop:
            pass
        print("instructions emitted:", cx.n_ins)
    return nc


def build_na_tables(rel_bias, ROWS):
    L = rel_bias.shape[0]
    tab = np.full((L, 8, 128, TABW), NEG, dtype=np.float32)
    p = np.arange(128)
    kr2 = p // 64
    kc = p % 64
    c = np.arange(64)
    cs = np.clip(c - 8, 0, GW - 16)
    colvalid = (kc[:, None] >= cs[None, :]) & (kc[:, None] < cs[None, :] + 16)
    dcol = np.clip(kc[:, None] - c[None, :] + 15, 0, 30)
    for a_ in range(22):
        drow = kr2 + 17 - a_
        rowvalid = (drow >= 3) & (drow <= 10)
        valid = rowvalid[:, None] & colvalid
        dr = np.clip(drow, 0, 14)
        vals = rel_bias[:, :, dr[:, None], dcol]
        blk = tab[:, :, :, a_ * 64:(a_ + 1) * 64]
        blk[:, :, valid] = vals[:, :, valid]
    for jj in range(4):
        for qr in range(4):
            drow = 2 * jj + kr2 - qr + 7
            vals = rel_bias[:, :, drow[:, None], dcol]
            o = 1408 + jj * 256 + qr * 64
            blk = tab[:, :, :, o:o + 64]
            blk[:, :, colvalid] = vals[:, :, colvalid]
    for jj in range(4):
        for qq in range(3):
            drow = 2 * jj + kr2 - qq + 2
            vals = rel_bias[:, :, drow[:, None], dcol]
            o = 1408 + 1024 + jj * 192 + qq * 64
            blk = tab[:, :, :, o:o + 64]
            blk[:, :, colvalid] = vals[:, :, colvalid]
    return tab


def rope_const(S):
    t = np.arange(S)
    row = (t // GW).astype(np.float32)
    col = (t % GW).astype(np.float32)
    inv = np.power(np.float32(10000.0), -np.arange(8, dtype=np.float32) / np.float32(8)).astype(np.float32)
    ang = np.concatenate([row[:, None] * inv, col[:, None] * inv], axis=-1).astype(np.float32)
    cos = np.cos(ang.astype(np.float64)).astype(np.float32).T
    sin = np.sin(ang.astype(np.float64)).astype(np.float32).T
    out = np.zeros((32, 2, S), np.float32)
    out[0:16, 0] = cos
    out[16:32, 0] = cos
    out[0:16, 1] = sin
    out[16:32, 1] = sin
    return out


def const_block():
    cst = np.zeros((128, 160), np.float32)
    cst[:, 0:128] = np.eye(128, dtype=np.float32)
    for i in range(16):
        cst[i + 16, 128 + i] = -1.0
        cst[i, 128 + i + 16] = 1.0
    return cst


def pack_shared(inputs, L, S):
    f = lambda a: np.ascontiguousarray(np.asarray(a, dtype=np.float32))
    vec = np.zeros((L, 128, NV), np.float32)
    p = np.arange(128)
    vec[:, :, 0:8] = f(inputs["ln_g"])[:L].reshape(L, 8, 128).transpose(0, 2, 1)
    vec[:, :, 8] = f(inputs["na_q_norm"])[:L][:, p % 64]
    vec[:, :, 9] = f(inputs["na_k_norm"])[:L][:, p % 64]
    vec[:, :, 10:12] = f(inputs["mla_cq_norm"])[:L].reshape(L, 2, 128).transpose(0, 2, 1)
    vec[:, :, 12] = f(inputs["mla_ckv_norm"])[:L]
    vec[:, 0:64, 13] = f(inputs["mla_q_norm"])[:L, 0:64]
    vec[:, 0:32, 14] = f(inputs["mla_q_norm"])[:L, 64:96]
    vec[:, 0:64, 15] = f(inputs["mla_k_norm"])[:L, 0:64]
    vec[:, 0:32, 16] = f(inputs["mla_k_norm"])[:L, 64:96]
    wukv = f(inputs["w_ukv"])[:L].reshape(L, 128, 8, 2, 64)
    wukv = np.ascontiguousarray(wukv.transpose(0, 1, 3, 2, 4)).reshape(L, 128, 1024)
    return {
        "w_in": f(inputs["w_in"])[:L], "w_uq": f(inputs["w_uq"])[:L], "w_ukv": wukv,
        "w_o_na": f(inputs["w_o_na"])[:L], "w_o_mla": f(inputs["w_o_mla"])[:L], "w_out": f(inputs["w_out"])[:L],
        "vecs": vec, "natab": build_na_tables(f(inputs["na_rel_bias"])[:L], S // GW),
        "ropecs": rope_const(S), "consts": const_block(),
    }


_CACHE = {}


def kernel(**inputs):
    x = np.asarray(inputs["x"], dtype=np.float32)
    B, S, _ = x.shape
    L = inputs["ln_g"].shape[0]
    NCORES = 8
    NB = B // NCORES
    key = (S, L, NB)
    if key not in _CACHE:
        _CACHE[key] = build_program(S, L, NB)
    nc = _CACHE[key]
    shared = pack_shared(inputs, L, S)
    in_maps = []
    for ci in range(NCORES):
        m = dict(shared)
        m["xT"] = np.ascontiguousarray(x[ci * NB:(ci + 1) * NB].transpose(0, 2, 1))
        in_maps.append(m)
    res = run_bass_kernel_spmd(nc, in_maps, core_ids=list(range(NCORES)))
    out = np.empty((B, S, D), np.float32)
    for ci in range(NCORES):
        out[ci * NB:(ci + 1) * NB] = np.asarray(res.results[ci]["outT"]).transpose(0, 2, 1)
    return out
```

```python
import contextlib
import os
import numpy as np
import concourse.bass as bass
import concourse.mybir as mybir
from concourse.bass_utils import run_bass_kernel_spmd

F32 = mybir.dt.float32
BF16 = mybir.dt.bfloat16
AF = mybir.ActivationFunctionType
ALU = mybir.AluOpType

D = 1024
DIN = 5024
GW = 64
EPS = 1e-6
NV = 17
TABW = 1408 + 1024 + 768
NEG = -30000.0


class Sem:
    def __init__(self, h):
        self.h = h
        self.total = 0


class Buf:
    __slots__ = ("w", "r", "dsem", "name", "excl")

    def __init__(self, name=""):
        self.excl = False
        self.w = None
        self.r = {}
        self.dsem = None
        self.name = name


class Ctx:
    def __init__(self, nc, es):
        self.nc = nc
        self.es = es
        self.engs = {"pe": nc.tensor, "act": nc.scalar, "dve": nc.vector, "pool": nc.gpsimd, "sp": nc.sync}
        self.esem = {k: Sem(es.enter_context(nc.semaphore("s_" + k))) for k in self.engs}
        self.waited = {k: {} for k in self.engs}
        self.dsems = []
        self.free_dsems = []
        self.bar = Sem(es.enter_context(nc.semaphore("s_bar")))
        self.n_ins = 0

    def _need(self, e, tok):
        if tok is None:
            return
        sem, cnt, eng = tok
        if eng == e:
            return
        target = cnt if eng is not None else sem.total
        w = self.waited[e]
        if w.get(id(sem), 0) >= target:
            return
        self.engs[e].wait_ge(sem.h, target)
        w[id(sem)] = target

    def _deps(self, e, reads, writes):
        for b in reads:
            self._need(e, b.w)
            if b.excl:
                for t in b.r.values():
                    self._need(e, t)
        for b in writes:
            self._need(e, b.w)
            for t in b.r.values():
                self._need(e, t)

    def _mark(self, tok, key, reads, writes):
        for b in reads:
            b.r[key] = tok
        for b in writes:
            b.w = tok
            b.r = {}

    def op(self, e, fn, reads=(), writes=(), sig=True):
        self._deps(e, reads, writes)
        ins = fn(self.engs[e])
        s = self.esem[e]
        if sig:
            s.total += 1
            ins.then_inc(s.h, 1)
            tok = (s, s.total, e)
        else:
            tok = (s, s.total + 1, e)
        self._mark(tok, e, reads, writes)
        self.n_ins += 1
        return ins

    def get_dsem(self, b):
        if b.dsem is None:
            if self.free_dsems:
                b.dsem = self.free_dsems.pop()
            else:
                b.dsem = Sem(self.es.enter_context(self.nc.semaphore("d%d" % len(self.dsems))))
                self.dsems.append(b.dsem)
        return b.dsem

    def release(self, bufs):
        for b in bufs:
            if b.dsem is not None:
                self.free_dsems.append(b.dsem)
                b.dsem = None

    def dma(self, q, out, in_, sb, load):
        if load:
            self._deps(q, (), (sb,))
        else:
            self._deps(q, (sb,), ())
        s = self.get_dsem(sb)
        ins = self.engs[q].dma_start(out=out, in_=in_)
        ins.then_inc(s.h, 16)
        s.total += 16
        tok = (s, s.total, None)
        if load:
            self._mark(tok, id(s), (), (sb,))
        else:
            self._mark(tok, id(s), (sb,), ())
        self.n_ins += 1

    def barrier(self, persistent=()):
        sp = self.engs["sp"]
        w = self.waited["sp"]
        allsems = [s for k, s in self.esem.items() if k != "sp"] + self.dsems
        for s in allsems:
            if w.get(id(s), 0) < s.total:
                sp.wait_ge(s.h, s.total)
                w[id(s)] = s.total
        self.bar.total += 1
        sp.sem_inc(self.bar.h, 1)
        for k in self.engs:
            if k == "sp":
                continue
            self.engs[k].wait_ge(self.bar.h, self.bar.total)
            ww = self.waited[k]
            for s in allsems:
                ww[id(s)] = s.total


_UID = [0]


class _Stop(Exception):
    pass


class Ring:
    def __init__(self, nc, es, name, n, shape, dtype, psum=False):
        self.tiles = []
        self.bufs = []
        for i in range(n):
            _UID[0] += 1
            nm = "%s_%d_%d" % (name, i, _UID[0])
            if psum:
                t = es.enter_context(nc.psum_tensor(nm, shape, dtype))
            else:
                t = es.enter_context(nc.sbuf_tensor(nm, shape, dtype))
            self.tiles.append(t)
            self.bufs.append(Buf("%s%d" % (name, i)))
        self.i = 0

    def next(self):
        t, b = self.tiles[self.i], self.bufs[self.i]
        self.i = (self.i + 1) % len(self.tiles)
        return t, b


def build_program(S, L, NB):
    NCH = S // 512
    NT = S // 128
    ROWS = S // GW
    VW = NT * 65
    nc = bass.Bass("TRN2", target_bir_lowering=False)

    def din(name, shape, dt=F32):
        return nc.dram_tensor(name, list(shape), dt, kind="ExternalInput").ap()

    def dscr(name, shape, dt=BF16):
        return nc.dram_tensor(name, list(shape), dt, kind="Internal").ap()

    xT = din("xT", [NB, D, S])
    w_in = din("w_in", [L, D, DIN])
    w_uq = din("w_uq", [L, 256, 768])
    w_ukv = din("w_ukv", [L, 128, 1024])
    w_o_na = din("w_o_na", [L, 512, D])
    w_o_mla = din("w_o_mla", [L, 512, D])
    w_out = din("w_out", [L, D, D])
    vecs = din("vecs", [L, 128, NV])
    natab = din("natab", [L, 8, 128, TABW])
    ropecs = din("ropecs", [32, 2, S])
    consts = din("consts", [128, 160])
    outT = nc.dram_tensor("outT", [NB, D, S], F32, kind="ExternalOutput").ap()

    xs = [[dscr("xs%d_%d" % (b_, i_), [D, S], F32) for i_ in range(2)] for b_ in range(NB)]
    qna_s = dscr("qna_s", [512, S]); kna_s = dscr("kna_s", [512, S]); gna_s = dscr("gna_s", [512, S])
    vna_s = dscr("vna_s", [8, 128, VW])
    qm_s = dscr("qm_s", [8, 96, S]); km_s = dscr("km_s", [8, 96, S]); gm_s = dscr("gm_s", [512, S])
    vm_s = dscr("vm_s", [8, 128, VW])
    sgn_s = dscr("sgn_s", [D, S]); sgm_s = dscr("sgm_s", [D, S])
    xn_s = dscr("xn_s", [D, S])
    ona_s = dscr("ona_s", [512, S]); om_s = dscr("om_s", [512, S])

    with contextlib.ExitStack() as es:
        es.enter_context(nc.allow_low_precision(reason="bf16 matmul operands, fp32 accumulation"))
        cx = Ctx(nc, es)
        op, dma = cx.op, cx.dma

        cst = es.enter_context(nc.sbuf_tensor("cst", [128, 160], BF16))
        cstb = Buf("cst")
        onesb = es.enter_context(nc.sbuf_tensor("onesb", [128, 128], BF16))
        blk = es.enter_context(nc.sbuf_tensor("blk", [128, 128], BF16))
        onesf = es.enter_context(nc.sbuf_tensor("onesf", [128, 64], F32))
        epsT = es.enter_context(nc.sbuf_tensor("epsT", [128, 1], F32))
        cb = Buf("consts")
        dma("pool", cst[:, :], consts, cstb, True)
        op("dve", lambda e: e.memset(onesb[:, :], 1.0), (), (cb,))
        op("dve", lambda e: e.memset(blk[:, :], 0.0), (), (cb,))
        op("dve", lambda e: e.memset(blk[0:64, 0:64], 1.0), (), (cb,))
        op("dve", lambda e: e.memset(blk[64:128, 64:128], 1.0), (), (cb,))
        op("dve", lambda e: e.memset(onesf[:, :], 1.0), (), (cb,))
        op("dve", lambda e: e.memset(epsT[:, :], EPS), (), (cb,))
        ident = cst[:, 0:128]
        rmat = cst[0:32, 128:160]

        psbig = [es.enter_context(nc.psum_tensor("psb%d" % i_, [128, 1024], F32)) for i_ in range(4)]

        class _PsRing:
            pass

        psr = _PsRing()
        psr.tiles = [psbig[i_ // 2][:, (i_ % 2) * 512:(i_ % 2 + 1) * 512] for i_ in range(8)]
        psr.bufs = [Buf("ps%d" % i_) for i_ in range(8)]
        psr.i = 0

        def _ps_next():
            t_, b_ = psr.tiles[psr.i], psr.bufs[psr.i]
            psr.i = (psr.i + 1) % 8
            return t_, b_

        psr.next = _ps_next
        for b_ in psr.bufs:
            b_.excl = True

        class SubRing:
            def __init__(self, lo, hi):
                self.tiles = psr.tiles[lo:hi]
                self.bufs = psr.bufs[lo:hi]
                self.i = 0

            def next(self):
                t, b = self.tiles[self.i], self.bufs[self.i]
                self.i = (self.i + 1) % len(self.tiles)
                return t, b

        persist = [cb, cstb] + psr.bufs
        psA = SubRing(0, 2)
        psS = SubRing(2, 6)
        psB = SubRing(6, 8)

        def rstd_from(ps_t, ps_b, M, nq, inv_n, lnr, rsr):
            lt, lb = lnr.next()
            op("act", lambda e: e.activation(out=lt[0:M, 0:nq], in_=ps_t[0:M, 0:nq], func=AF.Ln,
                                             bias=epsT[0:M, 0:1], scale=inv_n), (ps_b, cb), (lb,))
            rt, rb = rsr.next()
            op("act", lambda e: e.activation(out=rt[0:M, 0:nq], in_=lt[0:M, 0:nq], func=AF.Exp, scale=-0.5),
               (lb,), (rb,))
            return rt, rb

        ksub = int(os.environ.get("KSUB", "99"))
        stop_at = int(os.environ.get("KSTOP", "9999"))
        nphase = [0]

        def phase_done():
            nphase[0] += 1

        def active():
            return nphase[0] < stop_at

        try:
          for l in range(L):
              for b in range(NB):
                  xsrc = xT[b] if l == 0 else xs[b][(l - 1) % 2]
                  xdst = outT[b] if l == L - 1 else xs[b][l % 2]
                  xsrc_v = xsrc.rearrange("(k p) t -> p k t", p=128)
                  xdst_v = xdst.rearrange("(k p) t -> p k t", p=128)

                  for half in range(2):
                      if not active():
                          continue
                      if half == 0:
                          segs = [(0, 1536), (2048, 2464)]
                      else:
                          segs = [(1536, 2048), (2464, 5024)]
                      cw = sum(b_ - a_ for a_, b_ in segs)

                      def lcol(col, segs=segs):
                          o = 0
                          for a_, b_ in segs:
                              if a_ <= col < b_:
                                  return o + col - a_
                              o += b_ - a_
                          raise ValueError(col)
                      with contextlib.ExitStack() as ps_:
                          pbufs = []

                          def sb(name, shape, dt, n=1):
                              r = Ring(nc, ps_, name, n, shape, dt)
                              pbufs.extend(r.bufs)
                              return r

                          wsb = sb("w_in_sb", [128, 8, cw], BF16)
                          wt, wb = wsb.next()
                          w_in_v = w_in[l].rearrange("(k p) n -> p k n", p=128)
                          for k in range(8):
                              o_ = 0
                              for a_, b_ in segs:
                                  dma("pool", wt[:, k, o_:o_ + b_ - a_], w_in_v[:, k, a_:b_], wb, True)
                                  o_ += b_ - a_
                          vr = sb("vec", [128, NV], F32)
                          vt, vb = vr.next()
                          dma("sp", vt[:, :], vecs[l], vb, True)
                          op("dve", lambda e: e.tensor_scalar_mul(out=vt[:, 8:9], in0=vt[:, 8:9], scalar1=0.125), (), (vb,))
                          op("dve", lambda e: e.tensor_scalar_mul(out=vt[:, 13:15], in0=vt[:, 13:15],
                                                                  scalar1=float(96 ** -0.5)), (), (vb,))
                          xr = sb("xt", [128, 8, 512], F32)
                          sqr = sb("sq", [128, 8, 512], BF16)
                          xnr = sb("xn", [128, 8, 512], BF16, 2)
                          lnr = sb("lnt", [128, 512], F32, 2)
                          rsr = sb("rs", [128, 512], F32, 4)
                          rawr = sb("raw", [128, 512], F32, 4)
                          sqhr = sb("sqh", [128, 512], BF16, 4)
                          er = sb("etmp", [128, 512], F32, 3)
                          st4 = sb("st4", [128, 4, 512], BF16, 2)
                          if half == 0:
                              wuq_r = sb("wuq", [128, 2, 768], BF16)
                              wuq, wuqb = wuq_r.next()
                              wuq_v = w_uq[l].rearrange("(k p) n -> p k n", p=128)
                              dma("pool", wuq[:, :, :], wuq_v, wuqb, True)
                              wukv_r = sb("wukv", [128, 1024], BF16)
                              wukv, wukvb = wukv_r.next()
                              dma("pool", wukv[:, :], w_ukv[l], wukvb, True)
                              vstr = sb("vst", [128, 8, 4, 65], BF16, 2)
                              for t_, b_ in zip(vstr.tiles, vstr.bufs):
                                  op("dve", lambda e, t_=t_: e.memset(t_[:, :, :, 64:65], 1.0), (), (b_,))
                              crawr = sb("craw", [128, 2, 512], F32)
                              csqr = sb("csq", [128, 2, 512], BF16)
                              cqnr = sb("cqn", [128, 2, 512], BF16)
                              ckvnr = sb("ckvn", [128, 512], BF16)
                              stn = sb("stn", [64, 512], BF16, 4)
                              stp = sb("stp", [32, 512], BF16, 4)
                              csr = sb("cs", [32, 2, 512], F32)
                              p32r = sb("p32", [32, 512], F32, 4)
                              p16r = sb("p16", [32, 512], BF16, 4)
                              rwpr = sb("rwp", [32, 512], F32, 3)
                              sqpr = sb("sqp", [32, 512], BF16, 3)
                              g32r = sb("g32", [32, 512], F32, 3)
                              g16r = sb("g16", [32, 512], BF16, 3)
                              krr = sb("kr", [32, 512], F32)
                              sqper = sb("sqpe", [32, 512], BF16)

                          def prologue(c):
                              t0 = c * 512
                              if half == 1:
                                  xn, xnb = xnr.next()
                                  dma("sp", xn[:, :, :], xn_s[:, t0:t0 + 512].rearrange("(k p) t -> p k t", p=128), xnb, True)
                                  return xn, xnb
                              xt, xb = xr.next()
                              dma("sp", xt[:, :, :], xsrc_v[:, :, t0:t0 + 512], xb, True)
                              sq, sqb = sqr.next()
                              op("pool", lambda e: e.tensor_tensor(out=sq[:, :, :], in0=xt[:, :, :], in1=xt[:, :, :],
                                                                   op=ALU.mult), (xb,), (sqb,))
                              pss, pssb = psr.next()
                              for k in range(8):
                                  op("pe", lambda e, k=k: e.matmul(pss[:, :], onesb[:, :], sq[:, k, :],
                                                                   start=(k == 0), stop=(k == 7)),
                                     (sqb, cb), (pssb,), sig=(k == 7))
                              rx, rxb = rstd_from(pss, pssb, 128, 512, 1.0 / D, lnr, rsr)
                              xn, xnb = xnr.next()
                              for k in range(8):
                                  op("dve", lambda e, k=k: e.scalar_tensor_tensor(
                                      out=xn[:, k, :], in0=xt[:, k, :], scalar=vt[:, k:k + 1], in1=rx[:, :],
                                      op0=ALU.mult, op1=ALU.mult), (xb, vb, rxb), (xnb,))
                              dma("pool", xn_s[:, t0:t0 + 512].rearrange("(k p) t -> p k t", p=128), xn[:, :, :], xnb, False)
                              return xn, xnb

                          pro = prologue(0)
                          for c in range(NCH):
                              t0 = c * 512
                              xn, xnb = pro
                              if half == 0 and c + 1 < NCH:
                                  pro = prologue(c + 1)

                              def proj_fm(col, M, xn=xn, xnb=xnb):
                                  pt, pb = psr.next()
                                  for k in range(8):
                                      op("pe", lambda e, k=k: e.matmul(pt[0:M, :], wt[:, k, lcol(col):lcol(col) + M],
                                                                       xn[:, k, :], start=(k == 0), stop=(k == 7)),
                                         (wb, xnb), (pb,), sig=(k == 7))
                                  return pt, pb

                              def store4(stg, sgb, dst, row0, t0=t0):
                                  dv = dst[row0:row0 + 512, t0:t0 + 512].rearrange("(j p) t -> p j t", p=128)
                                  dma("pool", dv, stg[:, :, :], sgb, False)

                              def gate_chunks(colbase, nchunk, dst, silu):
                                  for g4 in range(nchunk // 4):
                                      stg, sgb = st4.next()
                                      for j in range(4):
                                          jj = g4 * 4 + j
                                          pt, pb = proj_fm(colbase + jj * 128, 128)
                                          if silu:
                                              et, eb = er.next()
                                              op("act", lambda e: e.activation(out=et[:, :], in_=pt[:, :], func=AF.Tanh,
                                                                               scale=0.5), (pb,), (eb,))
                                              op("dve", lambda e: e.scalar_tensor_tensor(
                                                  out=stg[:, j, :], in0=et[:, :], scalar=1.0, in1=pt[:, :],
                                                  op0=ALU.add, op1=ALU.mult), (pb, eb), (sgb,))
                                          else:
                                              op("act", lambda e: e.activation(out=stg[:, j, :], in_=pt[:, :], func=AF.Tanh,
                                                                               scale=0.5), (pb,), (sgb,))
                                      store4(stg, sgb, dst, g4 * 512)

                              if half == 1:
                                  if c + 1 < NCH:
                                      pro = prologue(c + 1)
                                  gate_chunks(1536, 4, gna_s, True)
                                  gate_chunks(2464, 4, gm_s, True)
                                  gate_chunks(2976, 8, sgn_s, False)
                                  gate_chunks(4000, 8, sgm_s, False)
                                  continue

                              cs, csb = csr.next()
                              dma("sp", cs[:, :, :], ropecs[:, :, t0:t0 + 512], csb, True)
                              tasks = []

                              def rope_pe(src16, src16b):
                                  pr, prb = psr.next()
                                  op("pe", lambda e: e.matmul(pr[0:32, :], rmat, src16[:, :], start=True, stop=True),
                                     (src16b, cstb), (prb,))
                                  return pr, prb

                              def rope_dve(src32, src32b, pr, prb, out_ap, outb):
                                  t1, t1b = p32r.next()
                                  op("dve", lambda e: e.tensor_tensor(out=t1[:, :], in0=src32[:, :], in1=cs[:, 0, :],
                                                                      op=ALU.mult), (src32b, csb), (t1b,))
                                  t2, t2b = p32r.next()
                                  op("dve", lambda e: e.tensor_tensor(out=t2[:, :], in0=pr[0:32, :], in1=cs[:, 1, :],
                                                                      op=ALU.mult), (prb, csb), (t2b,))
                                  op("dve", lambda e: e.tensor_tensor(out=out_ap, in0=t1[:, :], in1=t2[:, :], op=ALU.add),
                                     (t1b, t2b), (outb,))

                              craw, crawb = crawr.next()
                              csq, csqb = csqr.next()
                              cqn, cqnb = cqnr.next()

                              def cq_s1():
                                  for j in range(2):
                                      pt, pb = proj_fm(2048 + j * 128, 128)
                                      op("act", lambda e: e.activation(out=craw[:, j, :], in_=pt[:, :], func=AF.Copy), (pb,), (crawb,))
                                      op("act", lambda e: e.activation(out=csq[:, j, :], in_=pt[:, :], func=AF.Square), (pb,), (csqb,))

                              def cq_s2():
                                  p2, p2b = psr.next()
                                  for j in range(2):
                                      op("pe", lambda e, j=j: e.matmul(p2[:, :], onesb[:, :], csq[:, j, :], start=(j == 0), stop=(j == 1)),
                                         (csqb, cb), (p2b,), sig=(j == 1))
                                  rs, rsb = rstd_from(p2, p2b, 128, 512, 1.0 / 256, lnr, rsr)
                                  for j in range(2):
                                      op("dve", lambda e, j=j: e.scalar_tensor_tensor(
                                          out=cqn[:, j, :], in0=craw[:, j, :], scalar=vt[:, 10 + j:11 + j], in1=rs[:, :],
                                          op0=ALU.mult, op1=ALU.mult), (crawb, vb, rsb), (cqnb,))
                              tasks.append([cq_s1, cq_s2])

                              ckvn, ckvnb = ckvnr.next()
                              ckv_st = {}

                              def ckv_s1():
                                  pt, pb = proj_fm(2304, 128)
                                  rw, rwb = rawr.next()
                                  op("act", lambda e: e.activation(out=rw[:, :], in_=pt[:, :], func=AF.Copy), (pb,), (rwb,))
                                  sh, shb = sqhr.next()
                                  op("act", lambda e: e.activation(out=sh[:, :], in_=pt[:, :], func=AF.Square), (pb,), (shb,))
                                  ckv_st.update(rw=rw, rwb=rwb, sh=sh, shb=shb)

                              def ckv_s2():
                                  d_ = ckv_st
                                  p2, p2b = psr.next()
                                  op("pe", lambda e: e.matmul(p2[:, :], onesb[:, :], d_["sh"][:, :], start=True, stop=True), (d_["shb"], cb), (p2b,))
                                  rs, rsb = rstd_from(p2, p2b, 128, 512, 1.0 / 128, lnr, rsr)
                                  op("dve", lambda e: e.scalar_tensor_tensor(
                                      out=ckvn[:, :], in0=d_["rw"][:, :], scalar=vt[:, 12:13], in1=rs[:, :],
                                      op0=ALU.mult, op1=ALU.mult), (d_["rwb"], vb, rsb), (ckvnb,))
                              tasks.append([ckv_s1, ckv_s2])

                              sqpe, sqpeb = sqper.next()
                              kr, krb = krr.next()
                              kpe_st = {}

                              def kpe_s1():
                                  ppe, ppeb = proj_fm(2432, 32)
                                  op("act", lambda e: e.activation(out=sqpe[:, :], in_=ppe[0:32, :], func=AF.Square), (ppeb,), (sqpeb,))
                                  kg32, kg32b = g32r.next()
                                  op("dve", lambda e: e.tensor_scalar_mul(out=kg32[:, :], in0=ppe[0:32, :], scalar1=vt[0:32, 16:17]),
                                     (ppeb, vb), (kg32b,))
                                  kg16, kg16b = g16r.next()
                                  op("dve", lambda e: e.tensor_copy(out=kg16[:, :], in_=kg32[:, :]), (kg32b,), (kg16b,))
                                  kpe_st.update(g32=kg32, g32b=kg32b, g16=kg16, g16b=kg16b)

                              def kpe_s2():
                                  d_ = kpe_st
                                  pr, prb = rope_pe(d_["g16"], d_["g16b"])
                                  rope_dve(d_["g32"], d_["g32b"], pr, prb, kr[:, :], krb)
                              tasks.append([kpe_s1, kpe_s2])

                              for (colbase, gcol, dst) in ((0, 8, qna_s), (512, 9, kna_s)):
                                  grp = {}
                                  for j in range(4):
                                      d_ = {}

                                      def na_s1(d_=d_, j=j, colbase=colbase):
                                          pt, pb = proj_fm(colbase + j * 128, 128)
                                          rw, rwb = rawr.next()
                                          op("act", lambda e: e.activation(out=rw[:, :], in_=pt[:, :], func=AF.Copy), (pb,), (rwb,))
                                          sh, shb = sqhr.next()
                                          op("act", lambda e: e.activation(out=sh[:, :], in_=pt[:, :], func=AF.Square), (pb,), (shb,))
                                          d_.update(rw=rw, rwb=rwb, sh=sh, shb=shb)

                                      def na_s2(d_=d_, j=j, gcol=gcol, dst=dst, grp=grp):
                                          if j == 0:
                                              grp["stg"], grp["sgb"] = st4.next()
                                          stg, sgb = grp["stg"], grp["sgb"]
                                          p2, p2b = psr.next()
                                          op("pe", lambda e: e.matmul(p2[:, :], blk[:, :], d_["sh"][:, :], start=True, stop=True),
                                             (d_["shb"], cb), (p2b,))
                                          rs, rsb = rstd_from(p2, p2b, 128, 512, 1.0 / 64, lnr, rsr)
                                          op("dve", lambda e: e.scalar_tensor_tensor(
                                              out=stg[:, j, :], in0=d_["rw"][:, :], scalar=vt[:, gcol:gcol + 1], in1=rs[:, :],
                                              op0=ALU.mult, op1=ALU.mult), (d_["rwb"], vb, rsb), (sgb,))
                                          if j == 3:
                                              store4(stg, sgb, dst, 0)
                                      tasks.append([na_s1, na_s2])

                              vstn, vstnb = vstr.next()
                              for tt in range(4):
                                  def nav_s1(tt=tt, vst=vstn, vstb=vstnb, xn=xn, xnb=xnb, c=c):
                                      pt, pb = psr.next()
                                      for k in range(8):
                                          op("pe", lambda e, k=k: e.matmul(pt[:, :], xn[:, k, tt * 128:(tt + 1) * 128],
                                                                           wt[:, k, lcol(1024):lcol(1024) + 512], start=(k == 0), stop=(k == 7)),
                                             (wb, xnb), (pb,), sig=(k == 7))
                                      op("act", lambda e: e.activation(out=vst[:, :, tt, 0:64],
                                                                       in_=pt[:, :].rearrange("p (h e) -> p h e", e=64),
                                                                       func=AF.Copy), (pb,), (vstb,))
                                      if tt == 3:
                                          dma("pool", vna_s.rearrange("h p f -> p h f")[:, :, c * 260:(c + 1) * 260],
                                              vst[:, :, :, :].rearrange("p h t e -> p h (t e)"), vstb, False)
                                  tasks.append([nav_s1])

                              def head_stats(d_, sqp_t, sqp_b):
                                  p3, p3b = psr.next()
                                  op("pe", lambda e: e.matmul(p3[0:64, :], onesb[0:64, 0:64], d_["sh"][0:64, :], start=True, stop=False),
                                     (d_["shb"], cb), (p3b,), sig=False)
                                  op("pe", lambda e: e.matmul(p3[0:64, :], onesb[0:32, 0:64], sqp_t[:, :], start=False, stop=True),
                                     (sqp_b, cb), (p3b,))
                                  return rstd_from(p3, p3b, 64, 512, 1.0 / 96, lnr, rsr)

                              def nope_copy(d_, pn, pnb):
                                  rw, rwb = rawr.next()
                                  op("act", lambda e: e.activation(out=rw[0:64, :], in_=pn[0:64, :], func=AF.Copy), (pnb,), (rwb,))
                                  sh, shb = sqhr.next()
                                  op("act", lambda e: e.activation(out=sh[0:64, :], in_=pn[0:64, :], func=AF.Square), (pnb,), (shb,))
                                  d_.update(rw=rw, rwb=rwb, sh=sh, shb=shb)

                              for h in range(8):
                                  d_ = {}

                                  def q_s1(d_=d_, h=h):
                                      pn, pnb = psr.next()
                                      for j in range(2):
                                          op("pe", lambda e, j=j: e.matmul(pn[0:64, :], wuq[:, j, h * 96:h * 96 + 64], cqn[:, j, :],
                                                                           start=(j == 0), stop=(j == 1)),
                                             (wuqb, cqnb), (pnb,), sig=(j == 1))
                                      pp, ppb = psr.next()
                                      for j in range(2):
                                          op("pe", lambda e, j=j: e.matmul(pp[0:32, :], wuq[:, j, h * 96 + 64:h * 96 + 96], cqn[:, j, :],
                                                                           start=(j == 0), stop=(j == 1)),
                                             (wuqb, cqnb), (ppb,), sig=(j == 1))
                                      sqp, sqpb = sqpr.next()
                                      op("act", lambda e: e.activation(out=sqp[:, :], in_=pp[0:32, :], func=AF.Square), (ppb,), (sqpb,))
                                      rwp, rwpb = rwpr.next()
                                      op("act", lambda e: e.activation(out=rwp[:, :], in_=pp[0:32, :], func=AF.Copy), (ppb,), (rwpb,))
                                      nope_copy(d_, pn, pnb)
                                      d_.update(sqp=sqp, sqpb=sqpb, rwp=rwp, rwpb=rwpb)

                                  def q_s2(d_=d_, h=h, t0=t0):
                                      rs, rsb = head_stats(d_, d_["sqp"], d_["sqpb"])
                                      so, sob = stn.next()
                                      op("dve", lambda e: e.scalar_tensor_tensor(
                                          out=so[:, :], in0=d_["rw"][0:64, :], scalar=vt[0:64, 13:14], in1=rs[0:64, :],
                                          op0=ALU.mult, op1=ALU.mult), (d_["rwb"], vb, rsb), (sob,))
                                      dma("pool", qm_s[h, 0:64, t0:t0 + 512], so[:, :], sob, False)
                                      g32, g32b = g32r.next()
                                      op("dve", lambda e: e.scalar_tensor_tensor(
                                          out=g32[:, :], in0=d_["rwp"][:, :], scalar=vt[0:32, 14:15], in1=rs[0:32, :],
                                          op0=ALU.mult, op1=ALU.mult), (d_["rwpb"], vb, rsb), (g32b,))
                                      g16, g16b = g16r.next()
                                      op("dve", lambda e: e.tensor_copy(out=g16[:, :], in_=g32[:, :]), (g32b,), (g16b,))
                                      d_.update(g32=g32, g32b=g32b, g16=g16, g16b=g16b)

                                  def q_s3(d_=d_, h=h, t0=t0):
                                      pr, prb = rope_pe(d_["g16"], d_["g16b"])
                                      sp_, spb = stp.next()
                                      rope_dve(d_["g32"], d_["g32b"], pr, prb, sp_[:, :], spb)
                                      dma("pool", qm_s[h, 64:96, t0:t0 + 512], sp_[:, :], spb, False)
                                  tasks.append([q_s1, q_s2, q_s3])

                              for h in range(8):
                                  d_ = {}

                                  def k_s1(d_=d_, h=h):
                                      pn, pnb = psr.next()
                                      op("pe", lambda e: e.matmul(pn[0:64, :], wukv[:, h * 64:(h + 1) * 64], ckvn[:, :],
                                                                  start=True, stop=True), (wukvb, ckvnb), (pnb,))
                                      nope_copy(d_, pn, pnb)

                                  def k_s2(d_=d_, h=h, t0=t0):
                                      rs, rsb = head_stats(d_, sqpe, sqpeb)
                                      so, sob = stn.next()
                                      op("dve", lambda e: e.scalar_tensor_tensor(
                                          out=so[:, :], in0=d_["rw"][0:64, :], scalar=vt[0:64, 15:16], in1=rs[0:64, :],
                                          op0=ALU.mult, op1=ALU.mult), (d_["rwb"], vb, rsb), (sob,))
                                      dma("pool", km_s[h, 0:64, t0:t0 + 512], so[:, :], sob, False)
                                      sp_, spb = stp.next()
                                      op("dve", lambda e: e.tensor_tensor(out=sp_[:, :], in0=kr[:, :], in1=rs[0:32, :], op=ALU.mult),
                                         (krb, rsb), (spb,))
                                      dma("pool", km_s[h, 64:96, t0:t0 + 512], sp_[:, :], spb, False)
                                  tasks.append([k_s1, k_s2])

                              vstm, vstmb = vstr.next()
                              for tt in range(4):
                                  def mv_s1(tt=tt, vst=vstm, vstb=vstmb, c=c):
                                      pt, pb = psr.next()
                                      op("pe", lambda e: e.matmul(pt[:, :], ckvn[:, tt * 128:(tt + 1) * 128], wukv[:, 512:1024],
                                                                  start=True, stop=True), (wukvb, ckvnb), (pb,))
                                      op("act", lambda e: e.activation(out=vst[:, :, tt, 0:64],
                                                                       in_=pt[:, :].rearrange("p (h e) -> p h e", e=64),
                                                                       func=AF.Copy), (pb,), (vstb,))
                                      if tt == 3:
                                          dma("pool", vm_s.rearrange("h p f -> p h f")[:, :, c * 260:(c + 1) * 260],
                                              vst[:, :, :, :].rearrange("p h t e -> p h (t e)"), vstb, False)
                                  tasks.append([mv_s1])

                              nt_ = len(tasks)
                              for step in range(nt_ + 2):
                                  for s_ in range(3):
                                      i_ = step - s_
                                      if 0 <= i_ < nt_ and s_ < len(tasks[i_]):
                                          tasks[i_][s_]()

                          cx.barrier(persist)
                          cx.release(pbufs)
                          phase_done()

                  if active():
                      with contextlib.ExitStack() as ps_:
                          pbufs = []

                          def sb(name, shape, dt, n=1):
                              r = Ring(nc, ps_, name, n, shape, dt)
                              pbufs.extend(r.bufs)
                              return r

                          NI = ROWS // 8
                          R = {}
                          for nm, dk in (("m", 96), ("n", 64)):
                              R[nm] = dict(K=sb("K" + nm, [dk, S], BF16, 2), V=sb("V" + nm, [128, VW], BF16, 2),
                                           Q=sb("Q" + nm, [dk, S], BF16, 2), G=sb("G" + nm, [64, S], BF16, 2),
                                           pt=sb("pt" + nm, [128, 512], BF16, 4), rec=sb("rec" + nm, [65, 512], F32, 2),
                                           bc=sb("bc" + nm, [64, 512], F32, 2), ob=sb("ob" + nm, [64, 512], BF16, 3))
                          R["m"].update(acc=SubRing(0, 1), pt2=sb("pt2m", [128, 1024], BF16, 3))

                          class PairRing:
                              def __init__(self, pairs):
                                  self.pairs = pairs
                                  self.i = 0

                              def next(self):
                                  k_ = self.pairs[self.i]
                                  self.i = (self.i + 1) % len(self.pairs)
                                  return ((psr.tiles[2 * k_], psr.bufs[2 * k_]), (psr.tiles[2 * k_ + 1], psr.bufs[2 * k_ + 1]),
                                          psbig[k_])

                          R["m"]["stp"] = PairRing([2, 3])
                          R["n"].update(acc=SubRing(2, 3), st=SubRing(3, 4), depth=1)
                          bcps = SubRing(1, 2)
                          tbr = sb("tb", [128, TABW], F32, 2)
                          sbr = sb("sbias", [128, 512], F32, 3)

                          def finalize(rr, acc, accb, G, Gb, q0, nq, h, dst):
                              rec, recb = rr["rec"].next()
                              op("dve", lambda e: e.reciprocal(out=rec[64:65, 0:nq], in_=acc[64:65, 0:nq]), (accb,), (recb,))
                              pbc, pbcb = bcps.next()
                              op("pe", lambda e: e.matmul(pbc[0:64, 0:nq], onesf[64:65, 0:64], rec[64:65, 0:nq],
                                                          start=True, stop=True), (recb, cb), (pbcb,))
                              bc, bcb = rr["bc"].next()
                              op("dve", lambda e: e.scalar_tensor_tensor(out=bc[:, 0:nq], in0=pbc[0:64, 0:nq], scalar=0.5,
                                                                         in1=G[:, q0:q0 + nq], op0=ALU.mult, op1=ALU.mult),
                                 (pbcb, Gb), (bcb,))
                              ob, obb = rr["ob"].next()
                              op("dve", lambda e: e.tensor_tensor(out=ob[:, 0:nq], in0=acc[0:64, 0:nq], in1=bc[:, 0:nq],
                                                                  op=ALU.mult), (accb, bcb), (obb,))
                              dma("pool", dst[h * 64:(h + 1) * 64, q0:q0 + nq], ob[:, 0:nq], obb, False)

                          def mla_unit_gen(rr, K, Kb, V, Vb, Q, Qb, G, Gb, q0, h, dst):
                              acc, accb = rr["acc"].next()
                              npair = NT // 2
                              sts = {}

                              def emit_scores(ip):
                                  (sA, sAb), (sB, sBb), big = rr["stp"].next()
                                  for hf, (st, stb) in enumerate(((sA, sAb), (sB, sBb))):
                                      j = 2 * ip + hf
                                      op("pe", lambda e: e.matmul(st[:, :], K[:, j * 128:(j + 1) * 128], Q[:, q0:q0 + 512],
                                                                  start=True, stop=True), (Kb, Qb), (stb,))
                                  sts[ip] = (big, sAb, sBb)

                              def emit_pv(pend, last):
                                  ip, pt, ptb = pend
                                  for hf in range(2):
                                      j = 2 * ip + hf
                                      fin = last and hf == 1
                                      op("pe", lambda e: e.matmul(acc[0:65, :], V[:, j * 65:(j + 1) * 65], pt[:, hf * 512:(hf + 1) * 512],
                                                                  start=(j == 0), stop=fin), (Vb, ptb), (accb,), sig=fin)

                              emit_scores(0)
                              pend = None
                              for ip in range(npair):
                                  big, sAb, sBb = sts.pop(ip)
                                  pt, ptb = rr["pt2"].next()
                                  op("act", lambda e: e.activation(out=pt[:, :], in_=big[:, :], func=AF.Exp), (sAb, sBb), (ptb,))
                                  if ip + 1 < npair:
                                      emit_scores(ip + 1)
                                  if pend is not None:
                                      emit_pv(pend, False)
                                  pend = (ip, pt, ptb)
                                  yield
                              emit_pv(pend, True)
                              finalize(rr, acc, accb, G, Gb, q0, 512, h, dst)
                              yield

                          def unit_gen(rr, K, Kb, V, Vb, Q, Qb, G, Gb, tb, tbb, q0, nq, keys, h, dst):
                              acc, accb = rr["acc"].next()
                              nk = len(keys)
                              d = rr["depth"]
                              sts = {}

                              def emit_scores(i):
                                  j, off = keys[i]
                                  st, stb = rr["st"].next()
                                  op("pe", lambda e: e.matmul(st[:, 0:nq], K[:, j * 128:(j + 1) * 128], Q[:, q0:q0 + nq],
                                                              start=True, stop=True), (Kb, Qb), (stb,))
                                  if off is not None:
                                      sbt, sbb = sbr.next()
                                      op("dve", lambda e: e.tensor_tensor(out=sbt[:, 0:nq], in0=st[:, 0:nq], in1=tb[:, off:off + nq],
                                                                          op=ALU.add), (stb, tbb), (sbb,))
                                      st, stb = sbt, sbb
                                  sts[i] = (st, stb)

                              for i in range(min(d, nk)):
                                  emit_scores(i)
                              pend = None
                              for i in range(nk):
                                  if i not in sts:
                                      emit_scores(i)
                                  st, stb = sts.pop(i)
                                  pt, ptb = rr["pt"].next()
                                  op("act", lambda e: e.activation(out=pt[:, 0:nq], in_=st[:, 0:nq], func=AF.Exp), (stb,), (ptb,))
                                  if d > 0 and i + d < nk:
                                      emit_scores(i + d)
                                  if pend is not None:
                                      pi, ppt, pptb = pend
                                      pj = keys[pi][0]
                                      op("pe", lambda e: e.matmul(acc[0:65, 0:nq], V[:, pj * 65:(pj + 1) * 65], ppt[:, 0:nq],
                                                                  start=(pi == 0), stop=False), (Vb, pptb), (accb,), sig=False)
                                  pend = (i, pt, ptb)
                                  yield
                              pi, ppt, pptb = pend
                              pj = keys[pi][0]
                              op("pe", lambda e: e.matmul(acc[0:65, 0:nq], V[:, pj * 65:(pj + 1) * 65], ppt[:, 0:nq],
                                                          start=(pi == 0), stop=True), (Vb, pptb), (accb,))
                              finalize(rr, acc, accb, G, Gb, q0, nq, h, dst)
                              yield

                          def head_gen(nm, h):
                              rr = R[nm]
                              K, Kb = rr["K"].next()
                              V, Vb = rr["V"].next()
                              Q, Qb = rr["Q"].next()
                              G, Gb = rr["G"].next()
                              if nm == "m":
                                  dma("sp", K[:, :], km_s[h], Kb, True)
                                  dma("sp", V[:, :], vm_s[h], Vb, True)
                                  dma("sp", Q[:, :], qm_s[h], Qb, True)
                                  dma("sp", G[:, :], gm_s[h * 64:(h + 1) * 64, :], Gb, True)
                                  tb, tbb = None, None
                                  dst = om_s
                                  units = [(c * 512, 512, [(j, None) for j in range(NT)]) for c in range(NCH)]
                              else:
                                  dma("sp", K[:, :], kna_s[h * 64:(h + 1) * 64, :], Kb, True)
                                  dma("sp", V[:, :], vna_s[h], Vb, True)
                                  dma("sp", Q[:, :], qna_s[h * 64:(h + 1) * 64, :], Qb, True)
                                  dma("sp", G[:, :], gna_s[h * 64:(h + 1) * 64, :], Gb, True)
                                  tb, tbb = tbr.next()
                                  dma("sp", tb[:, :], natab[l, h], tbb, True)
                                  dst = ona_s
                                  units = [(0, 256, [(j, 1408 + j * 256) for j in range(4)])]
                                  for i in range(NI):
                                      qr0 = 4 if i == 0 else 0
                                      qr1 = 5 if i == NI - 1 else 8
                                      keys = []
                                      for j in range(max(0, 4 * i - 2), min(ROWS // 2 - 1, 4 * i + 5) + 1):
                                          dlt = j - 4 * i
                                          keys.append((j, (qr0 - 2 * dlt + 10) * 64))
                                      units.append(((8 * i + qr0) * 64, (qr1 - qr0) * 64, keys))
                                  units.append(((ROWS - 3) * 64, 192,
                                                [(ROWS // 2 - 4 + jj, 1408 + 1024 + jj * 192) for jj in range(4)]))
                              for (q0, nq, keys) in units:
                                  if nm == "m":
                                      g_ = mla_unit_gen(rr, K, Kb, V, Vb, Q, Qb, G, Gb, q0, h, dst)
                                  else:
                                      g_ = unit_gen(rr, K, Kb, V, Vb, Q, Qb, G, Gb, tb, tbb, q0, nq, keys, h, dst)
                                  for _ in g_:
                                      yield

                          n_m = NCH * (NT // 2 + 1)
                          n_n = sum(1 for _ in [0]) * 0 + (4 + 1) * 2 + sum(
                              (min(ROWS // 2 - 1, 4 * i + 5) - max(0, 4 * i - 2) + 1) + 1 for i in range(NI))
                          ratio = max(1, int(round(n_m / float(n_n))))
                          for h in range(8):
                              gm = head_gen("m", h)
                              gn = head_gen("n", h)
                              alive_m = alive_n = True
                              while alive_m or alive_n:
                                  for _ in range(ratio):
                                      if alive_m and next(gm, "end") == "end":
                                          alive_m = False
                                  if alive_n and next(gn, "end") == "end":
                                      alive_n = False

                          cx.barrier(persist)
                          cx.release(pbufs)
                          phase_done()

                  if not active():
                      continue
                  with contextlib.ExitStack() as ps_:
                      pbufs = []

                      def sb(name, shape, dt, n=1):
                          r = Ring(nc, ps_, name, n, shape, dt)
                          pbufs.extend(r.bufs)
                          return r

                      wona, wonab = sb("wona", [64, 8, D], BF16).next()
                      wom, womb = sb("wom", [64, 8, D], BF16).next()
                      wout, woutb = sb("wout", [128, 8, D], BF16).next()
                      dma("pool", wona[:, :, :], w_o_na[l].rearrange("(h d) n -> d h n", d=64), wonab, True)
                      dma("pool", wom[:, :, :], w_o_mla[l].rearrange("(h d) n -> d h n", d=64), womb, True)
                      wout_v = w_out[l].rearrange("(k p) n -> p k n", p=128)
                      for k in range(8):
                          dma("pool", wout[:, k, :], wout_v[:, k, :], woutb, True)
                      onar = sb("ona", [64, 8, 512], BF16, 2)
                      omr = sb("om", [64, 8, 512], BF16, 2)
                      sgnr = sb("sgn", [128, 8, 512], BF16, 2)
                      sgmr = sb("sgm", [128, 8, 512], BF16, 2)
                      xr = sb("xt4", [128, 8, 512], F32, 2)
                      yr = sb("y", [128, 8, 512], BF16, 2)
                      t1r = sb("t1", [128, 512], F32, 3)
                      t2r = sb("t2", [128, 512], F32, 3)
                      for c in range(NCH):
                          t0 = c * 512
                          ona, onab = onar.next()
                          om, omb = omr.next()
                          sgn, sgnb = sgnr.next()
                          sgm, sgmb = sgmr.next()
                          xt, xb = xr.next()
                          dma("sp", ona[:, :, :], ona_s[:, t0:t0 + 512].rearrange("(h d) t -> d h t", d=64), onab, True)
                          dma("sp", om[:, :, :], om_s[:, t0:t0 + 512].rearrange("(h d) t -> d h t", d=64), omb, True)
                          dma("sp", sgn[:, :, :], sgn_s[:, t0:t0 + 512].rearrange("(k p) t -> p k t", p=128), sgnb, True)
                          dma("sp", sgm[:, :, :], sgm_s[:, t0:t0 + 512].rearrange("(k p) t -> p k t", p=128), sgmb, True)
                          dma("sp", xt[:, :, :], xsrc_v[:, :, t0:t0 + 512], xb, True)
                          y, yb = yr.next()
                          for n in range(8):
                              pu1, pu1b = psr.next()
                              for h in range(8):
                                  op("pe", lambda e, h=h: e.matmul(pu1[:, :], wona[:, h, n * 128:(n + 1) * 128], ona[:, h, :],
                                                                   start=(h == 0), stop=(h == 7)), (wonab, onab), (pu1b,), sig=(h == 7))
                              pu2, pu2b = psr.next()
                              for h in range(8):
                                  op("pe", lambda e, h=h: e.matmul(pu2[:, :], wom[:, h, n * 128:(n + 1) * 128], om[:, h, :],
                                                                   start=(h == 0), stop=(h == 7)), (womb, omb), (pu2b,), sig=(h == 7))
                              t1, t1b = t1r.next()
                              op("dve", lambda e: e.scalar_tensor_tensor(out=t1[:, :], in0=sgn[:, n, :], scalar=1.0, in1=pu1[:, :],
                                                                         op0=ALU.add, op1=ALU.mult), (pu1b, sgnb), (t1b,))
                              t2, t2b = t2r.next()
                              op("dve", lambda e: e.scalar_tensor_tensor(out=t2[:, :], in0=sgm[:, n, :], scalar=1.0, in1=pu2[:, :],
                                                                         op0=ALU.add, op1=ALU.mult), (pu2b, sgmb), (t2b,))
                              op("pool", lambda e: e.tensor_tensor(out=y[:, n, :], in0=t1[:, :], in1=t2[:, :], op=ALU.add),
                                 (t1b, t2b), (yb,))
                          for m in range(8):
                              po, pob = psr.next()
                              for n in range(8):
                                  op("pe", lambda e, n=n: e.matmul(po[:, :], wout[:, n, m * 128:(m + 1) * 128], y[:, n, :],
                                                                   start=(n == 0), stop=(n == 7)), (woutb, yb), (pob,), sig=(n == 7))
                              op("dve", lambda e: e.scalar_tensor_tensor(out=xt[:, m, :], in0=po[:, :], scalar=0.5, in1=xt[:, m, :],
                                                                         op0=ALU.mult, op1=ALU.add), (pob,), (xb,))
                          dma("pool", xdst_v[:, :, t0:t0 + 512], xt[:, :, :], xb, False)
                      cx.barrier(persist)
                      cx.release(pbufs)
                      phase_done()
        except _Stop:
            pass
        print("instructions emitted:", cx.n_ins)
    return nc


def build_na_tables(rel_bias, ROWS):
    L = rel_bias.shape[0]
    tab = np.full((L, 8, 128, TABW), NEG, dtype=np.float32)
    p = np.arange(128)
    kr2 = p // 64
    kc = p % 64
    c = np.arange(64)
    cs = np.clip(c - 8, 0, GW - 16)
    colvalid = (kc[:, None] >= cs[None, :]) & (kc[:, None] < cs[None, :] + 16)
    dcol = np.clip(kc[:, None] - c[None, :] + 15, 0, 30)
    for a_ in range(22):
        drow = kr2 + 17 - a_
        rowvalid = (drow >= 3) & (drow <= 10)
        valid = rowvalid[:, None] & colvalid
        dr = np.clip(drow, 0, 14)
        vals = rel_bias[:, :, dr[:, None], dcol]
        blk = tab[:, :, :, a_ * 64:(a_ + 1) * 64]
        blk[:, :, valid] = vals[:, :, valid]
    for jj in range(4):
        for qr in range(4):
            drow = 2 * jj + kr2 - qr + 7
            vals = rel_bias[:, :, drow[:, None], dcol]
            o = 1408 + jj * 256 + qr * 64
            blk = tab[:, :, :, o:o + 64]
            blk[:, :, colvalid] = vals[:, :, colvalid]
    for jj in range(4):
        for qq in range(3):
            drow = 2 * jj + kr2 - qq + 2
            vals = rel_bias[:, :, drow[:, None], dcol]
            o = 1408 + 1024 + jj * 192 + qq * 64
            blk = tab[:, :, :, o:o + 64]
            blk[:, :, colvalid] = vals[:, :, colvalid]
    return tab


def rope_const(S):
    t = np.arange(S)
    row = (t // GW).astype(np.float32)
    col = (t % GW).astype(np.float32)
    inv = np.power(np.float32(10000.0), -np.arange(8, dtype=np.float32) / np.float32(8)).astype(np.float32)
    ang = np.concatenate([row[:, None] * inv, col[:, None] * inv], axis=-1).astype(np.float32)
    cos = np.cos(ang.astype(np.float64)).astype(np.float32).T
    sin = np.sin(ang.astype(np.float64)).astype(np.float32).T
    out = np.zeros((32, 2, S), np.float32)
    out[0:16, 0] = cos
    out[16:32, 0] = cos
    out[0:16, 1] = sin
    out[16:32, 1] = sin
    return out


def const_block():
    cst = np.zeros((128, 160), np.float32)
    cst[:, 0:128] = np.eye(128, dtype=np.float32)
    for i in range(16):
        cst[i + 16, 128 + i] = -1.0
        cst[i, 128 + i + 16] = 1.0
    return cst


def pack_shared(inputs, L, S):
    f = lambda a: np.ascontiguousarray(np.asarray(a, dtype=np.float32))
    vec = np.zeros((L, 128, NV), np.float32)
    p = np.arange(128)
    vec[:, :, 0:8] = f(inputs["ln_g"])[:L].reshape(L, 8, 128).transpose(0, 2, 1)
    vec[:, :, 8] = f(inputs["na_q_norm"])[:L][:, p % 64]
    vec[:, :, 9] = f(inputs["na_k_norm"])[:L][:, p % 64]
    vec[:, :, 10:12] = f(inputs["mla_cq_norm"])[:L].reshape(L, 2, 128).transpose(0, 2, 1)
    vec[:, :, 12] = f(inputs["mla_ckv_norm"])[:L]
    vec[:, 0:64, 13] = f(inputs["mla_q_norm"])[:L, 0:64]
    vec[:, 0:32, 14] = f(inputs["mla_q_norm"])[:L, 64:96]
    vec[:, 0:64, 15] = f(inputs["mla_k_norm"])[:L, 0:64]
    vec[:, 0:32, 16] = f(inputs["mla_k_norm"])[:L, 64:96]
    wukv = f(inputs["w_ukv"])[:L].reshape(L, 128, 8, 2, 64)
    wukv = np.ascontiguousarray(wukv.transpose(0, 1, 3, 2, 4)).reshape(L, 128, 1024)
    return {
        "w_in": f(inputs["w_in"])[:L], "w_uq": f(inputs["w_uq"])[:L], "w_ukv": wukv,
        "w_o_na": f(inputs["w_o_na"])[:L], "w_o_mla": f(inputs["w_o_mla"])[:L], "w_out": f(inputs["w_out"])[:L],
        "vecs": vec, "natab": build_na_tables(f(inputs["na_rel_bias"])[:L], S // GW),
        "ropecs": rope_const(S), "consts": const_block(),
    }


_CACHE = {}


def kernel(**inputs):
    x = np.asarray(inputs["x"], dtype=np.float32)
    B, S, _ = x.shape
    L = inputs["ln_g"].shape[0]
    NCORES = 8
    NB = B // NCORES
    key = (S, L, NB)
    if key not in _CACHE:
        _CACHE[key] = build_program(S, L, NB)
    nc = _CACHE[key]
    shared = pack_shared(inputs, L, S)
    in_maps = []
    for ci in range(NCORES):
        m = dict(shared)
        m["xT"] = np.ascontiguousarray(x[ci * NB:(ci + 1) * NB].transpose(0, 2, 1))
        in_maps.append(m)
    res = run_bass_kernel_spmd(nc, in_maps, core_ids=list(range(NCORES)))
    out = np.empty((B, S, D), np.float32)
    for ci in range(NCORES):
        out[ci * NB:(ci + 1) * NB] = np.asarray(res.results[ci]["outT"]).transpose(0, 2, 1)
    return out
```
